# Optimizing a Trainium2 kernel written in Bass

```python
import jax
import jax.numpy as jnp
from jax import lax
import numpy as np

D_MODEL = 2048
BATCH = 8
SEQ = 4096
DEPTH = 2

CTX_LEN = 256
GRID_W = 64
N_MOD = 9
D_FF = 5632
FFN_RES_WEIGHT = 0.5
EPS = 1e-6
POS_THETA = 10000.0
CHUNK = 64

D_GLA = D_MODEL // 2
GLA_HEADS = 4
GLA_DK = D_GLA // (2 * GLA_HEADS)
GLA_DV = D_GLA // GLA_HEADS
GLA_LOWRANK = 16
GLA_TAU = 16.0
GLA_LOG_DECAY_MIN = -1.0
D_RG = D_MODEL // 2
RG_BLOCKS = 8
RG_BLOCK = D_RG // RG_BLOCKS
RG_C = 8.0
CONV_W = 4
CONV_LEFT = 2
D_ML = D_MODEL
ML_HEADS = 8
ML_DV = D_ML // ML_HEADS
ML_DK = ML_DV // 2

AB_SIZES = (GLA_HEADS * GLA_DK, GLA_HEADS * GLA_DK, D_GLA, D_GLA, 2 * GLA_LOWRANK, D_RG, D_RG)
ML_SIZES = (ML_HEADS * ML_DK, ML_HEADS * ML_DK, D_ML, D_ML, 4 * ML_HEADS)
AB_COLS = sum(AB_SIZES)
ML_COLS = sum(ML_SIZES)
N_EVEN = (DEPTH + 1) // 2
N_ODD = DEPTH // 2

kernel_name = "hybrid_gla_rglru_mlstm_macaron_dit"


def rmsnorm(x, g):
    x32 = x.astype(jnp.float32)
    y = x32 * lax.rsqrt(jnp.mean(x32 * x32, axis=-1, keepdims=True) + EPS)
    return (y * g.astype(jnp.float32)).astype(x.dtype)


def modulate(h, g, shift, scale):
    return rmsnorm(h, g) * (1.0 + scale) + shift


def add_residual(h, y, g, gate, weight):
    return h + weight * gate * rmsnorm(y, g)


def swiglu(u, w_in, w_out):
    a, b = jnp.split(u @ w_in, 2, axis=-1)
    return (jax.nn.silu(a) * b) @ w_out


def ffn_sub(h, mod, ng, w_in, w_out, slot):
    u = modulate(h, ng[2 * slot], mod[:, :, 3 * slot], mod[:, :, 3 * slot + 1])
    return add_residual(h, swiglu(u, w_in, w_out), ng[2 * slot + 1], mod[:, :, 3 * slot + 2], FFN_RES_WEIGHT)


def split_cols(t, sizes):
    idx = []
    acc = 0
    for s in sizes[:-1]:
        acc += s
        idx.append(acc)
    return jnp.split(t, idx, axis=-1)


def to_heads(t, n_heads):
    b, l, _ = t.shape
    return t.reshape(b, l, n_heads, -1).transpose(0, 2, 1, 3)


def from_heads(t):
    b, h, l, d = t.shape
    return t.transpose(0, 2, 1, 3).reshape(b, l, h * d)


def to_chunks(t):
    n = t.shape[2] // CHUNK
    return jnp.moveaxis(t.reshape(t.shape[:2] + (n, CHUNK) + t.shape[3:]), 2, 0)


def from_chunks(t):
    t = jnp.moveaxis(t, 0, 2)
    return t.reshape(t.shape[:2] + (-1,) + t.shape[4:])


def pos_embed_2d(rows, dtype):
    row = jnp.repeat(jnp.arange(rows), GRID_W)
    col = jnp.tile(jnp.arange(GRID_W), rows)
    n_freq = D_MODEL // 4
    omega = POS_THETA ** (-jnp.arange(n_freq, dtype=jnp.float32) / n_freq)

    def enc(p):
        ang = p.astype(jnp.float32)[:, None] * omega[None, :]
        return jnp.concatenate([jnp.sin(ang), jnp.cos(ang)], axis=-1)

    return jnp.concatenate([enc(row), enc(col)], axis=-1).astype(dtype)


def dwconv_centred(t, w, b):
    l = t.shape[1]
    tp = jnp.pad(t, ((0, 0), (CONV_LEFT, CONV_W - 1 - CONV_LEFT), (0, 0)))
    y = b
    for tap in range(CONV_W):
        y = y + tp[:, tap:tap + l] * w[tap]
    return y


def blockdiag(t, w):
    b, l, _ = t.shape
    return jnp.einsum('blnc,ncd->blnd', t.reshape(b, l, RG_BLOCKS, RG_BLOCK), w).reshape(b, l, -1)


def lin_combine(left, right):
    a_l, b_l = left
    a_r, b_r = right
    return a_l * a_r, a_r * b_l + b_r


def rglru_dir(xb, w_a, b_a, w_i, b_i, lam, h0):
    f32 = jnp.float32
    r = jax.nn.sigmoid(blockdiag(xb, w_a.astype(f32)) + b_a.astype(f32))
    i = jax.nn.sigmoid(blockdiag(xb, w_i.astype(f32)) + b_i.astype(f32))
    log_a = -RG_C * r * jax.nn.softplus(-lam.astype(f32))
    a = jnp.exp(log_a)
    bx = jnp.sqrt(-jnp.expm1(2.0 * log_a)) * (i * xb)
    bx = bx.at[:, 0].add(a[:, 0] * h0)
    _, h = lax.associative_scan(lin_combine, (a, bx), axis=1)
    return h


def gla_scan(q, k, v, log_a, s0, readout):
    tril = jnp.tril(jnp.ones((CHUNK, CHUNK), dtype=bool))

    def body(s, inp):
        qc, kc, vc, lc = inp
        b = jnp.cumsum(lc, axis=2)
        b_last = b[:, :, -1:, :]
        s_new = jnp.exp(b_last[:, :, 0, :, None]) * s + jnp.einsum('bhjd,bhjv->bhdv', kc * jnp.exp(b_last - b), vc)
        if not readout:
            return s_new, None
        b_ref = b[:, :, CHUNK // 2:CHUNK // 2 + 1, :]
        scores = jnp.einsum('bhid,bhjd->bhij', qc * jnp.exp(b - b_ref), kc * jnp.exp(b_ref - b))
        scores = jnp.where(tril, scores, 0.0)
        o = jnp.einsum('bhij,bhjv->bhiv', scores, vc) + jnp.einsum('bhid,bhdv->bhiv', qc * jnp.exp(b), s)
        return s_new, o

    s_fin, o = lax.scan(body, s0, (to_chunks(q), to_chunks(k), to_chunks(v), to_chunks(log_a)))
    if not readout:
        return None, s_fin
    return from_chunks(o), s_fin


def mlstm_scan(q, k, v, i_pre, log_f, state, readout):
    tril = jnp.tril(jnp.ones((CHUNK, CHUNK), dtype=bool))

    def body(carry, inp):
        cm, nm, mm = carry
        qc, kc, vc, ic, fc = inp
        b = jnp.cumsum(fc, axis=-1)
        b_last = b[..., -1]
        g = b_last[..., None] - b + ic
        m_new = jnp.maximum(b_last + mm, jnp.max(g, axis=-1))
        wk = jnp.exp(g - m_new[..., None])
        decay = jnp.exp(b_last + mm - m_new)
        c_new = decay[..., None, None] * cm + jnp.einsum('bhj,bhjd,bhjv->bhdv', wk, kc, vc)
        n_new = decay[..., None] * nm + jnp.einsum('bhj,bhjd->bhd', wk, kc)
        if not readout:
            return (c_new, n_new, m_new), None
        d_log = jnp.where(tril, b[..., :, None] - b[..., None, :] + ic[..., None, :], -jnp.inf)
        inter = b + mm[..., None]
        m_row = jnp.maximum(inter, jnp.max(d_log, axis=-1))
        w_intra = jnp.exp(d_log - m_row[..., None])
        w_inter = jnp.exp(inter - m_row)
        s = jnp.einsum('bhid,bhjd->bhij', qc, kc) * w_intra
        num = jnp.einsum('bhij,bhjv->bhiv', s, vc) + w_inter[..., None] * jnp.einsum('bhid,bhdv->bhiv', qc, cm)
        den = jnp.sum(s, axis=-1) + w_inter * jnp.einsum('bhid,bhd->bhi', qc, nm)
        h = num / jnp.maximum(jnp.abs(den), jnp.exp(-m_row))[..., None]
        return (c_new, n_new, m_new), h

    st_fin, h = lax.scan(body, state, (to_chunks(q), to_chunks(k), to_chunks(v), to_chunks(i_pre), to_chunks(log_f)))
    if not readout:
        return None, st_fin
    return from_chunks(h), st_fin


def mixer_ab(u, state, w_in, w_alpha2, b_alpha, gla_g, conv_w, conv_b, w_a, b_a, w_i, b_i, lam, w_out, readout):
    f32 = jnp.float32
    s_f0, s_b0, r_f0, r_b0 = state
    q, k, v, g, lr, xr, gr = split_cols(u @ w_in, AB_SIZES)
    qh = to_heads(q.astype(f32), GLA_HEADS) * GLA_DK ** -0.5
    kh = to_heads(k.astype(f32), GLA_HEADS)
    vh = to_heads(v.astype(f32), GLA_HEADS)
    lr_f, lr_b = jnp.split(lr.astype(f32), 2, axis=-1)

    def log_decay(lr_d, d):
        z = lr_d @ w_alpha2[d].astype(f32) + b_alpha[d].astype(f32)
        return to_heads(jnp.maximum(jax.nn.log_sigmoid(z) / GLA_TAU, GLA_LOG_DECAY_MIN), GLA_HEADS)

    o_f, s_f = gla_scan(qh, kh, vh, log_decay(lr_f, 0), s_f0, readout)
    o_b, s_b = gla_scan(jnp.flip(qh, 2), jnp.flip(kh, 2), jnp.flip(vh, 2), jnp.flip(log_decay(lr_b, 1), 2), s_b0, readout)
    xb = dwconv_centred(xr, conv_w, conv_b).astype(f32)
    h_f = rglru_dir(xb, w_a[0], b_a[0], w_i[0], b_i[0], lam[0], r_f0)
    h_b = jnp.flip(rglru_dir(jnp.flip(xb, 1), w_a[1], b_a[1], w_i[1], b_i[1], lam[1], r_b0), 1)
    new_state = (s_f, s_b, h_f[:, -1], h_b[:, 0])
    if not readout:
        return None, new_state
    o = rmsnorm(o_f + jnp.flip(o_b, 2), gla_g)
    y_gla = from_heads(o).astype(u.dtype) * jax.nn.silu(g)
    y_rg = (h_f + h_b).astype(u.dtype) * jax.nn.gelu(gr)
    return jnp.concatenate([y_gla, y_rg], axis=-1) @ w_out, new_state


def mixer_c(u, state, w_in, b_gates, norm_g_h, w_out, readout):
    f32 = jnp.float32
    st_f0, st_b0 = state
    q, k, v, o, gates = split_cols(u @ w_in, ML_SIZES)
    gates = jnp.transpose(gates.astype(f32) + b_gates.astype(f32), (0, 2, 1))
    i_f, f_f, i_b, f_b = jnp.split(gates, 4, axis=1)
    qh = to_heads(q.astype(f32), ML_HEADS) * ML_DK ** -0.5
    kh = to_heads(k.astype(f32), ML_HEADS)
    vh = to_heads(v.astype(f32), ML_HEADS)
    h_f, st_f = mlstm_scan(qh, kh, vh, i_f, jax.nn.log_sigmoid(f_f), st_f0, readout)
    h_b, st_b = mlstm_scan(jnp.flip(qh, 2), jnp.flip(kh, 2), jnp.flip(vh, 2), jnp.flip(i_b, 2),
                           jnp.flip(jax.nn.log_sigmoid(f_b), 2), st_b0, readout)
    if not readout:
        return None, (st_f, st_b)
    h = rmsnorm(h_f + jnp.flip(h_b, 2), norm_g_h)
    return (jax.nn.sigmoid(o) * from_heads(h).astype(u.dtype)) @ w_out, (st_f, st_b)


def setup_inputs(seed: int = 0) -> dict:
    key = jax.random.key(seed)
    ks = jax.random.split(key, 32)
    f32 = jnp.float32

    def nrm(k, shape, scale):
        return scale * jax.random.normal(k, shape, f32)

    s = jax.random.uniform(ks[20], (N_EVEN, 2, D_RG), f32, 0.9, 0.999) ** (1.0 / RG_C)
    rg_lambda = jnp.log(s) - jnp.log1p(-s)
    ml_b_gates = jnp.concatenate([
        nrm(ks[24], (N_ODD, ML_HEADS), 0.1), 3.0 + nrm(ks[25], (N_ODD, ML_HEADS), 0.5),
        nrm(ks[26], (N_ODD, ML_HEADS), 0.1), 3.0 + nrm(ks[27], (N_ODD, ML_HEADS), 0.5)], axis=-1)
    return {
        "x": nrm(ks[0], (BATCH, SEQ, D_MODEL), 1.0),
        "c": nrm(ks[1], (BATCH, D_MODEL), 1.0),
        "ctx": nrm(ks[2], (BATCH, CTX_LEN, D_MODEL), 1.0),
        "c_ctx": nrm(ks[3], (D_MODEL,), 1.0),
        "w_mod": nrm(ks[4], (DEPTH, D_MODEL, N_MOD * D_MODEL), 0.5 * D_MODEL ** -0.5),
        "b_mod": nrm(ks[5], (DEPTH, N_MOD * D_MODEL), 0.01),
        "norm_g": 1.0 + nrm(ks[6], (DEPTH, 6, D_MODEL), 0.05),
        "ffn1_w_in": nrm(ks[7], (DEPTH, D_MODEL, 2 * D_FF), D_MODEL ** -0.5),
        "ffn1_w_out": nrm(ks[8], (DEPTH, D_FF, D_MODEL), D_FF ** -0.5),
        "ffn2_w_in": nrm(ks[9], (DEPTH, D_MODEL, 2 * D_FF), D_MODEL ** -0.5),
        "ffn2_w_out": nrm(ks[10], (DEPTH, D_FF, D_MODEL), D_FF ** -0.5),
        "ab_w_in": nrm(ks[11], (N_EVEN, D_MODEL, AB_COLS), D_MODEL ** -0.5),
        "gla_w_alpha2": nrm(ks[12], (N_EVEN, 2, GLA_LOWRANK, GLA_HEADS * GLA_DK), GLA_LOWRANK ** -0.5),
        "gla_b_alpha": nrm(ks[13], (N_EVEN, 2, GLA_HEADS * GLA_DK), 0.01),
        "gla_norm_g": 1.0 + nrm(ks[14], (N_EVEN, GLA_DV), 0.05),
        "rg_conv_w": nrm(ks[15], (N_EVEN, CONV_W, D_RG), CONV_W ** -0.5),
        "rg_conv_b": nrm(ks[16], (N_EVEN, D_RG), 0.01),
        "rg_w_a": nrm(ks[17], (N_EVEN, 2, RG_BLOCKS, RG_BLOCK, RG_BLOCK), RG_BLOCK ** -0.5),
        "rg_b_a": nrm(ks[18], (N_EVEN, 2, D_RG), 0.01),
        "rg_w_i": nrm(ks[19], (N_EVEN, 2, RG_BLOCKS, RG_BLOCK, RG_BLOCK), RG_BLOCK ** -0.5),
        "rg_b_i": nrm(ks[21], (N_EVEN, 2, D_RG), 0.01),
        "rg_lambda": rg_lambda,
        "ab_w_out": nrm(ks[22], (N_EVEN, D_GLA + D_RG, D_MODEL), (D_GLA + D_RG) ** -0.5),
        "ml_w_in": nrm(ks[23], (N_ODD, D_MODEL, ML_COLS), D_MODEL ** -0.5),
        "ml_b_gates": ml_b_gates,
        "ml_norm_g": 1.0 + nrm(ks[28], (N_ODD, ML_DV), 0.05),
        "ml_w_out": nrm(ks[29], (N_ODD, D_ML, D_MODEL), D_ML ** -0.5),
    }


def reference(x, c, ctx, c_ctx, w_mod, b_mod, norm_g, ffn1_w_in, ffn1_w_out, ffn2_w_in, ffn2_w_out,
              ab_w_in, gla_w_alpha2, gla_b_alpha, gla_norm_g, rg_conv_w, rg_conv_b, rg_w_a, rg_b_a,
              rg_w_i, rg_b_i, rg_lambda, ab_w_out, ml_w_in, ml_b_gates, ml_norm_g, ml_w_out):
    f32 = jnp.float32
    bsz, n_lat, _ = x.shape
    ROWS = n_lat // GRID_W
    h_lat = x + pos_embed_2d(ROWS, x.dtype)[None]
    h_ctx = ctx
    gla_zero = jnp.zeros((bsz, GLA_HEADS, GLA_DK, GLA_DV), f32)
    rg_zero = jnp.zeros((bsz, D_RG), f32)
    ml_zero = (jnp.zeros((bsz, ML_HEADS, ML_DK, ML_DV), f32), jnp.zeros((bsz, ML_HEADS, ML_DK), f32),
               jnp.zeros((bsz, ML_HEADS), f32))
    for layer in range(DEPTH):
        last = layer == DEPTH - 1
        ng = norm_g[layer]
        mod_lat = (jax.nn.silu(c) @ w_mod[layer] + b_mod[layer]).reshape(bsz, 1, N_MOD, D_MODEL)
        mod_ctx = (jax.nn.silu(c_ctx) @ w_mod[layer] + b_mod[layer]).reshape(1, 1, N_MOD, D_MODEL)
        h_ctx = ffn_sub(h_ctx, mod_ctx, ng, ffn1_w_in[layer], ffn1_w_out[layer], 0)
        h_lat = ffn_sub(h_lat, mod_lat, ng, ffn1_w_in[layer], ffn1_w_out[layer], 0)
        u_ctx = modulate(h_ctx, ng[2], mod_ctx[:, :, 3], mod_ctx[:, :, 4])
        u_lat = modulate(h_lat, ng[2], mod_lat[:, :, 3], mod_lat[:, :, 4])
        j = layer // 2
        if layer % 2 == 0:
            weights = (ab_w_in[j], gla_w_alpha2[j], gla_b_alpha[j], gla_norm_g[j], rg_conv_w[j], rg_conv_b[j],
                       rg_w_a[j], rg_b_a[j], rg_w_i[j], rg_b_i[j], rg_lambda[j], ab_w_out[j])
            y_ctx, ctx_state = mixer_ab(u_ctx, (gla_zero, gla_zero, rg_zero, rg_zero), *weights, readout=not last)
            y_lat, _ = mixer_ab(u_lat, ctx_state, *weights, readout=True)
        else:
            weights = (ml_w_in[j], ml_b_gates[j], ml_norm_g[j], ml_w_out[j])
            y_ctx, ctx_state = mixer_c(u_ctx, (ml_zero, ml_zero), *weights, readout=not last)
            y_lat, _ = mixer_c(u_lat, ctx_state, *weights, readout=True)
        h_lat = add_residual(h_lat, y_lat, ng[3], mod_lat[:, :, 5], 1.0)
        h_lat = ffn_sub(h_lat, mod_lat, ng, ffn2_w_in[layer], ffn2_w_out[layer], 2)
        if not last:
            h_ctx = add_residual(h_ctx, y_ctx, ng[3], mod_ctx[:, :, 5], 1.0)
            h_ctx = ffn_sub(h_ctx, mod_ctx, ng, ffn2_w_in[layer], ffn2_w_out[layer], 2)
    return h_lat
```

```python
import contextlib
import math
import os
import numpy as np
import concourse.bass as bass
import concourse.mybir as mybir
from concourse.bass_utils import run_bass_kernel_spmd

F32 = mybir.dt.float32
BF16 = mybir.dt.bfloat16
I32 = mybir.dt.int32
AF = mybir.ActivationFunctionType
ALU = mybir.AluOpType

ENGS = ("pe", "dve", "act", "pool", "sp")
NDMASEM = 48

D = 2048
KC = 16
NCTX = 256
NLAT = 4096
T = NCTX + NLAT
DFF = 5632
EPS = 1e-6
AB_COLS = 5152
ML_COLS = 6176


class Op:
    __slots__ = ("eng", "fn", "deps", "signal", "count", "is_dma", "dsem", "idx")

    def __init__(self, eng, fn, is_dma=False):
        self.eng = eng
        self.fn = fn
        self.deps = []
        self.signal = False
        self.count = None
        self.is_dma = is_dma
        self.dsem = None


class Prog:
    def __init__(self, nc):
        self.nc = nc
        self.ops = {e: [] for e in ENGS}
        self.last_w = {}
        self.readers = {}
        self.ndma = 0
        self.dma_ops = []
        self.nops = 0

    def _add(self, eng, fn, reads, writes, is_dma=False, extra=()):
        op = Op(eng, fn, is_dma)
        op.idx = self.nops
        self.nops += 1
        deps = list(extra)
        for r in reads:
            w = self.last_w.get(r)
            if w is not None:
                deps.append(w)
        for w_ in writes:
            w = self.last_w.get(w_)
            if w is not None:
                deps.append(w)
            deps.extend(self.readers.get(w_, ()))
        for w_ in writes:
            self.last_w[w_] = op
            self.readers[w_] = []
        for r in reads:
            self.readers.setdefault(r, []).append(op)
        if is_dma:
            k = self.ndma
            self.ndma += 1
            if k >= NDMASEM:
                deps.append(self.dma_ops[k - NDMASEM])
            self.dma_ops.append(op)
        seen = set()
        for d in deps:
            if d is op or id(d) in seen:
                continue
            seen.add(id(d))
            if (not d.is_dma) and d.eng == eng and eng == "pe":
                continue
            op.deps.append(d)
            d.signal = True
        self.ops[eng].append(op)
        return op

    def pe(self, fn, reads=(), writes=()):
        return self._add("pe", fn, reads, writes)

    def dve(self, fn, reads=(), writes=()):
        return self._add("dve", fn, reads, writes)

    def act(self, fn, reads=(), writes=()):
        return self._add("act", fn, reads, writes)

    def pool(self, fn, reads=(), writes=()):
        return self._add("pool", fn, reads, writes)

    def dma(self, eng, out, in_, reads=(), writes=(), **kw):
        return self._add(eng, lambda e: e.dma_start(out=out, in_=in_, **kw), reads, writes, is_dma=True)

    def emit(self, final_waits=("sp",)):
        nc = self.nc
        with contextlib.ExitStack() as st:
            esem = {e: st.enter_context(nc.semaphore("s_" + e)) for e in ENGS if e != "sp"}
            dsem = [st.enter_context(nc.semaphore("d%d" % i)) for i in range(NDMASEM)]
            block = st.enter_context(nc.Block())
            for e in ENGS:
                c = 0
                for op in self.ops[e]:
                    if op.is_dma:
                        continue
                    if op.signal:
                        c += 1
                        op.count = c
            dcount = [0] * NDMASEM
            for k, op in enumerate(self.dma_ops):
                s = k % NDMASEM
                dcount[s] += 16
                op.dsem = s
                op.count = dcount[s]

            def run(e_name, eng):
                waited = {}
                for op in self.ops[e_name]:
                    for d in op.deps:
                        if d.is_dma:
                            key = ("d", d.dsem)
                            sem = dsem[d.dsem]
                        else:
                            key = ("e", d.eng)
                            sem = esem[d.eng]
                        if waited.get(key, 0) >= d.count:
                            continue
                        waited[key] = d.count
                        eng.wait_ge(sem, d.count)
                    ins = op.fn(eng)
                    if op.is_dma:
                        ins.then_inc(dsem[op.dsem], 16)
                    elif op.signal:
                        ins.then_inc(esem[e_name], 1)
                if e_name in final_waits:
                    for k in range(NDMASEM):
                        if dcount[k] and waited.get(("d", k), 0) < dcount[k]:
                            eng.wait_ge(dsem[k], dcount[k])

            @block.tensor
            def _(eng):
                run("pe", eng)

            @block.vector
            def _(eng):
                run("dve", eng)

            @block.scalar
            def _(eng):
                run("act", eng)

            @block.gpsimd
            def _(eng):
                run("pool", eng)

            @block.sync
            def _(eng):
                run("sp", eng)


def fm(v):
    v = np.asarray(v)
    sh = v.shape
    v2 = v.reshape(sh[:-1] + (sh[-1] // 128, 128))
    return np.ascontiguousarray(np.moveaxis(v2, -1, 0)).reshape(128, -1)


PK = {}
_o = 0
for _n, _w in (("c", 16), ("cctx", 16), ("bmod", 2 * 144), ("ng", 2 * 6 * 16), ("balpha", 8),
               ("convw", 32), ("convb", 8), ("rgba", 16), ("rgbi", 16), ("rglam", 16), ("mlbg", 1)):
    PK[_n] = (_o, _w)
    _o += _w
NPK = _o

SBW = 46400

STAGE = os.environ.get("MK_STAGE", "full")


def build():
    nc = bass.Bass("TRN2", target_bir_lowering=False)
    dt_in = lambda n, s, d=F32: nc.dram_tensor(n, s, d, kind="ExternalInput").ap()
    dt_sc = lambda n, s, d: nc.dram_tensor(n, s, d, kind="Internal").ap()
    x_d = dt_in("x", [NLAT, D])
    ctx_d = dt_in("ctx", [NCTX, D])
    pk_d = dt_in("pk", [128, NPK])
    wa2_d = dt_in("wa2", [16, 1024])
    rgw_d = dt_in("rgw", [128, 4096])
    gn4_d = dt_in("gn4", [128, 1024])
    mn8_d = dt_in("mn8", [128, 2048])
    wmod_d = dt_in("w_mod", [2, D, 9 * D])
    f1in_d = dt_in("ffn1_w_in", [2, D, 2 * DFF])
    f1out_d = dt_in("ffn1_w_out", [2, DFF, D])
    f2in_d = dt_in("ffn2_w_in", [2, D, 2 * DFF])
    f2out_d = dt_in("ffn2_w_out", [2, DFF, D])
    abin_d = dt_in("ab_w_in", [D, AB_COLS])
    about_d = dt_in("ab_w_out", [D, D])
    mlin_d = dt_in("ml_w_in", [D, ML_COLS])
    mlout_d = dt_in("ml_w_out", [D, D])
    out_d = nc.dram_tensor("out", [NLAT, D], F32, kind="ExternalOutput").ap()

    f1in_b = [dt_sc("f1in_b%d" % l, [D, 2 * DFF], BF16) for l in range(2)]
    f1out_b = [dt_sc("f1out_b%d" % l, [DFF, D], BF16) for l in range(2)]
    f2in_b = [dt_sc("f2in_b%d" % l, [D, 2 * DFF], BF16) for l in range(2)]
    f2out_b = [dt_sc("f2out_b%d" % l, [DFF, D], BF16) for l in range(2)]
    abin_b = dt_sc("abin_b", [D, AB_COLS], BF16)
    about_b = dt_sc("about_b", [D, D], BF16)
    mlin_b = dt_sc("mlin_b", [D, ML_COLS], BF16)
    mlout_b = dt_sc("mlout_b", [D, D], BF16)
    hfm_d = dt_sc("hfm", [D, T], F32)
    ymix_d = dt_sc("ymix", [D, T], BF16)

    with contextlib.ExitStack() as st:
        SB = st.enter_context(nc.sbuf_tensor("SB", [128, SBW], F32))
        PS = [st.enter_context(nc.psum_tensor("ps%d" % i, [128, 512], F32)) for i in range(7)]
        PSB = st.enter_context(nc.psum_tensor("psb", [128, 1024], BF16))
        p = Prog(nc)

        top = [0]

        def alloc(nwords):
            o = top[0]
            top[0] += nwords
            assert top[0] <= SBW, ("SBUF arena overflow", top[0])
            return o

        def V(off, nwords, dt=F32, pat=None, **kw):
            ap = SB[:, off:off + nwords]
            if dt is not F32:
                ap = ap.bitcast(dt)
            if pat:
                ap = ap.rearrange(pat, **kw)
            return ap

        o_identf = alloc(128); identf = V(o_identf, 128)
        o_identb = alloc(64); identb = V(o_identb, 64, BF16)
        o_onesb = alloc(64); onesb = V(o_onesb, 64, BF16)
        o_pk = alloc(NPK); pk = V(o_pk, NPK)
        o_modraw = alloc(576); modraw = V(o_modraw, 576, F32, "p (l n w) -> p l n w", l=2, w=2)
        o_TA = alloc(192); TA = V(o_TA, 192, F32, "p (l w s c) -> p l w s c", l=2, w=2, s=3)
        o_TS = alloc(192); TS = V(o_TS, 192, F32, "p (l w s c) -> p l w s c", l=2, w=2, s=3)
        o_TG = alloc(192); TG = V(o_TG, 192, F32, "p (l w s c) -> p l w s c", l=2, w=2, s=3)
        o_scb = alloc(16); scb = V(o_scb, 16, BF16, "p (k w) -> p k w", w=2)
        o_postab = alloc(512); postab = V(o_postab, 512, F32, "p (c j) -> p c j", j=64)
        o_scr = alloc(8); scr = V(o_scr, 8)
        o_tmpf = alloc(128); tmpf = V(o_tmpf, 128)

        def pkc(name, j=0, n=None):
            o, w = PK[name]
            n = w - j if n is None else n
            return pk[:, o + j:o + j + n]

        base_top = top[0]

        p.dma("sp", pk, pk_d[:, :], writes=["pk"])
        p.pool(lambda e: e.memset(identf, 0.0), writes=["identf"])
        p.pool(lambda e: e.affine_select(out=identf, in_=identf, pattern=[[-1, 128]], compare_op=ALU.not_equal,
                                         fill=1.0, base=0, channel_multiplier=1), reads=["identf"], writes=["identf"])
        p.dve(lambda e: e.tensor_copy(out=identb, in_=identf), reads=["identf"], writes=["identb"])
        p.dve(lambda e: e.memset(onesb, 1.0), writes=["onesb"])
        p.dve(lambda e: e.memset(scr, 0.0), writes=["scr"])

        conv_prev = []

        def convert(src, dst, cc, key):
            R, C = src.shape
            per = C // cc
            rows = max(1, 2048 // per)
            r0 = 0
            keys = []
            i = 0
            while r0 < R:
                r1 = min(R, r0 + rows)
                s_ = src[r0:r1, :].rearrange("r (a c) -> r a c", c=cc)
                d_ = dst[r0:r1, :].rearrange("r (a c) -> r a c", c=cc)
                k = "%s_%d" % (key, i)
                p.dma("pool", d_, s_, reads=list(conv_prev[-6:-5]), writes=[k])
                conv_prev.append(k)
                keys.append(k)
                r0 = r1
                i += 1
            return keys

        wkeys = {}
        wkeys["f1in0"] = convert(f1in_d[0], f1in_b[0], 1408, "f1in0")
        wkeys["f1out0"] = convert(f1out_d[0], f1out_b[0], 2048, "f1out0")

        def conv_rest():
            wkeys["abin"] = convert(abin_d, abin_b, 1288, "abin")
            wkeys["about"] = convert(about_d, about_b, 2048, "about")
            wkeys["f2in0"] = convert(f2in_d[0], f2in_b[0], 1408, "f2in0")
            wkeys["f2out0"] = convert(f2out_d[0], f2out_b[0], 2048, "f2out0")
            wkeys["f1in1"] = convert(f1in_d[1], f1in_b[1], 1408, "f1in1")
            wkeys["f1out1"] = convert(f1out_d[1], f1out_b[1], 2048, "f1out1")
            wkeys["mlin"] = convert(mlin_d, mlin_b, 1544, "mlin")
            wkeys["mlout"] = convert(mlout_d, mlout_b, 2048, "mlout")
            wkeys["f2in1"] = convert(f2in_d[1], f2in_b[1], 1408, "f2in1")
            wkeys["f2out1"] = convert(f2out_d[1], f2out_b[1], 2048, "f2out1")

        p.act(lambda e: e.activation(out=scb[:, :, 0], in_=pkc("c"), func=AF.Silu), reads=["pk"], writes=["scb0"])
        p.act(lambda e: e.activation(out=scb[:, :, 1], in_=pkc("cctx"), func=AF.Silu), reads=["pk"], writes=["scb1"])
        o_wf = [alloc(8192) for _ in range(4)]
        WF = [V(o, 8192, F32, "p (k n) -> p k n", n=1024) for o in o_wf]
        o_wm = alloc(8192)
        WM = V(o_wm, 8192, BF16, "p (k n) -> p k n", n=1024)
        PSM = PS[6]
        wfc = [0]
        for l in range(2):
            wv = wmod_d[l].rearrange("(k p) n -> p k n", p=128)
            for g in range(18):
                wmk = []
                for kh in range(2):
                    b = wfc[0] % 4
                    wfc[0] += 1
                    p.dma("sp", WF[b], wv[:, kh * 8:(kh + 1) * 8, g * 1024:(g + 1) * 1024], writes=["WF%d" % b])
                    for k4 in range(2):
                        o_ = WM[:, kh * 8 + k4 * 4:kh * 8 + (k4 + 1) * 4, :]
                        i_ = WF[b][:, k4 * 4:(k4 + 1) * 4, :]
                        key = "WM_%d_%d" % (kh, k4)
                        wmk.append(key)
                        if k4 % 2 == 0:
                            p.dve(lambda e, o_=o_, i_=i_: e.tensor_copy(out=o_, in_=i_), reads=["WF%d" % b], writes=[key])
                        else:
                            p.act(lambda e, o_=o_, i_=i_: e.activation(out=o_, in_=i_, func=AF.Copy), reads=["WF%d" % b], writes=[key])
                for j in range(8):
                    blk = g * 8 + j
                    for k in range(KC):
                        first = k == 0
                        last = k == KC - 1
                        p.pe(lambda e, j=j, k=k, blk=blk, first=first, last=last:
                             e.matmul(PSM[:, blk * 2:blk * 2 + 2], WM[:, k, j * 128:(j + 1) * 128], scb[:, k, :],
                                      start=first, stop=last),
                             reads=(wmk + ["scb0", "scb1"]) if (first or last) else (),
                             writes=["PSM"] if ((first and blk == 0) or (last and blk == 143)) else ())
            bm = pkc("bmod", l * 144, 144)
            bm_b = bass.AP(bm.tensor, bm.offset, [list(bm.ap[0]), [1, 144], [0, 2]])
            p.dve(lambda e, l=l, bm_b=bm_b: e.tensor_tensor(out=modraw[:, l, :, :],
                                                          in0=PSM[:, 0:288].rearrange("p (n w) -> p n w", w=2),
                                                          in1=bm_b, op=ALU.add),
                  reads=["PSM", "pk"], writes=["modraw"])
        for l in range(2):
            for w in range(2):
                for s in range(3):
                    ngpre = pkc("ng", (l * 6 + 2 * s) * 16, 16)
                    ngpost = pkc("ng", (l * 6 + 2 * s + 1) * 16, 16)
                    wgt = 1.0 if s == 1 else 0.5
                    p.dve(lambda e, l=l, w=w, s=s, ngpre=ngpre: e.scalar_tensor_tensor(
                        out=TA[:, l, w, s, :], in0=modraw[:, l, (3 * s + 1) * 16:(3 * s + 2) * 16, w], scalar=1.0,
                        in1=ngpre, op0=ALU.add, op1=ALU.mult), reads=["modraw", "pk"], writes=["TA"])
                    p.dve(lambda e, l=l, w=w, s=s: e.tensor_copy(
                        out=TS[:, l, w, s, :], in_=modraw[:, l, (3 * s) * 16:(3 * s + 1) * 16, w]),
                        reads=["modraw"], writes=["TS"])
                    p.dve(lambda e, l=l, w=w, s=s, ngpost=ngpost, wgt=wgt: e.scalar_tensor_tensor(
                        out=TG[:, l, w, s, :], in0=modraw[:, l, (3 * s + 2) * 16:(3 * s + 3) * 16, w], scalar=wgt,
                        in1=ngpost, op0=ALU.mult, op1=ALU.mult), reads=["modraw", "pk"], writes=["TG"])

        o_pt = alloc(2048)
        fidx_i = V(o_pt, 4, I32)
        posi = V(o_pt + 8, 64, I32)
        fidx = V(o_pt + 80, 4)
        posf = V(o_pt + 96, 64)
        omega = V(o_pt + 160, 4)
        ang = V(o_pt + 256, 256, F32, "p (c j) -> p c j", j=64)
        tt = V(o_pt + 512, 256)
        ki = V(o_pt + 768, 256, I32)
        kf = V(o_pt + 1024, 256)
        w1 = V(o_pt + 1280, 256)
        gg = V(o_pt + 1536, 256)
        p.pool(lambda e: e.iota(fidx_i, pattern=[[128, 4]], base=0, channel_multiplier=1), writes=["fidx_i"])
        p.pool(lambda e: e.iota(posi, pattern=[[1, 64]], base=0, channel_multiplier=0), writes=["posi"])
        p.dve(lambda e: e.tensor_copy(out=fidx, in_=fidx_i), reads=["fidx_i"], writes=["fidx"])
        p.dve(lambda e: e.tensor_copy(out=posf, in_=posi), reads=["posi"], writes=["posf"])
        p.act(lambda e: e.activation(out=omega, in_=fidx, func=AF.Exp, scale=-math.log(10000.0) / 512.0),
              reads=["fidx"], writes=["omega"])
        for j in range(4):
            p.dve(lambda e, j=j: e.tensor_scalar(ang[:, j, :], posf, omega[:, j:j + 1], None, ALU.mult),
                  reads=["posf", "omega"], writes=["ang"])
        angf = V(o_pt + 256, 256)
        for half in range(2):
            p.dve(lambda e, half=half: e.tensor_scalar(tt, angf, 1.0 / (2 * math.pi), 0.25 * half, ALU.mult, ALU.add),
                  reads=["ang"], writes=["tt"])
            p.dve(lambda e: e.tensor_copy(out=ki, in_=tt), reads=["tt"], writes=["ki"])
            p.dve(lambda e: e.tensor_copy(out=kf, in_=ki), reads=["ki"], writes=["kf"])
            p.dve(lambda e: e.tensor_tensor(out=w1, in0=tt, in1=kf, op=ALU.subtract), reads=["tt", "kf"], writes=["w1"])
            p.dve(lambda e: e.tensor_single_scalar(gg, w1, 0.5, ALU.is_gt), reads=["w1"], writes=["gg"])
            p.dve(lambda e: e.tensor_tensor(out=w1, in0=w1, in1=gg, op=ALU.subtract), reads=["w1", "gg"], writes=["w1"])
            p.dve(lambda e: e.tensor_single_scalar(gg, w1, -0.5, ALU.is_lt), reads=["w1"], writes=["gg"])
            p.dve(lambda e: e.tensor_tensor(out=w1, in0=w1, in1=gg, op=ALU.add), reads=["w1", "gg"], writes=["w1"])
            p.act(lambda e, half=half: e.activation(out=postab[:, half * 4:(half + 1) * 4, :],
                                                    in_=w1.rearrange("p (c j) -> p c j", j=64), func=AF.Sin,
                                                    scale=2 * math.pi * (1 - 1e-6)),
                  reads=["w1"], writes=["postab"])

        def barrier():
            lasts = []
            for e in ENGS:
                for op in reversed(p.ops[e]):
                    if not op.is_dma:
                        lasts.append(op)
                        break
            ex = lasts + p.dma_ops[-NDMASEM:]
            p._add("dve", lambda e: e.memset(scr[:, 0:1], 0.0), (), ("_b0",), extra=ex)
            p._add("act", lambda e: e.activation(out=scr[:, 1:2], in_=scr[:, 2:3], func=AF.Copy), (), ("_b1",), extra=ex)
            p._add("pool", lambda e: e.memset(scr[:, 3:4], 0.0), (), ("_b2",), extra=ex)
            p._add("pe", lambda e: e.matmul(PS[6][0:2, 0:2], scb[:, 0, :], scb[:, 0, :], start=True, stop=True),
                   (), ("_b3",), extra=ex)
            p._add("sp", lambda e: e.dma_start(out=scr[0:1, 5:7], in_=pk_d[0:1, 0:2]), (), ("_b4",), is_dma=True, extra=ex)
            p.last_w = {k: v for k, v in p.last_w.items() if isinstance(k, str) and k.startswith(("hfm", "ymix", "f1", "f2", "ab", "ml", "sc_"))}
            p.readers = {k: v for k, v in p.readers.items() if k in p.last_w}
            for k in ("_b0", "_b1", "_b2", "_b3"):
                pass
            p._bar = [p.ops["dve"][-1], p.ops["act"][-1], p.ops["pool"][-1], p.ops["pe"][-1]]

        barrier()
        top[0] = base_top

        o_H = alloc(8192); H = V(o_H, 8192, F32, "p (c n) -> p c n", n=512)
        o_Y = alloc(8192); Y = V(o_Y, 8192, F32, "p (c n) -> p c n", n=512)
        XB = [V(o_Y + i * 2048, 2048) for i in range(2)]
        o_U = alloc(4096); U = V(o_U, 4096, BF16, "p (c n) -> p c n", n=512)
        o_HID = alloc(5632); HID = V(o_HID, 5632, BF16, "p (f n) -> p f n", n=512)
        o_WI = [alloc(4096), alloc(4096)]
        WI = [V(o, 4096, BF16, "p (k n) -> p k n", n=512) for o in o_WI]
        o_WO = [alloc(2816), alloc(2816)]
        WO = [V(o, 2816, BF16, "p (f n) -> p f n", n=256) for o in o_WO]
        o_RS = alloc(512); RS = V(o_RS, 512)
        o_RS2 = alloc(512); RS2 = V(o_RS2, 512)
        o_TMP = [alloc(512), alloc(512)]; TMP = [V(o, 512) for o in o_TMP]
        o_SA = [alloc(512), alloc(512)]; SA = [V(o, 512) for o in o_SA]
        o_SQ = [alloc(256), alloc(256)]; SQ = [V(o, 256, BF16) for o in o_SQ]
        wictr = [0]
        woctr = [0]
        psctr = [0]

        tiles = [(0, NCTX, 1)] + [(NCTX + 512 * i, 512, 0) for i in range(8)]

        def bar_deps():
            return p._bar

        first_after_bar = {e: True for e in ENGS}

        def stats_chunk(src, rkeys, N, c):
            PSS = PS[6]
            b = c % 2
            p.act(lambda e, c=c, b=b: e.activation(out=SQ[b][:, :N], in_=src(c), func=AF.Square),
                  reads=rkeys(c), writes=["SQ%d" % b])
            p.pe(lambda e, c=c, b=b: e.matmul(PSS[:, :N], onesb, SQ[b][:, :N], start=(c == 0), stop=(c == KC - 1)),
                 reads=["SQ%d" % b, "onesb"], writes=["PSS"] if c in (0, KC - 1) else ())

        def stats_finish(N, rs_out, rs_key):
            PSS = PS[6]
            p.act(lambda e: e.activation(out=TMP[0][:, :N], in_=PSS[:, :N], func=AF.Sqrt, bias=scr[:, 4:5], scale=1.0 / D),
                  reads=["PSS", "eps"], writes=["TMP0"])
            p.dve(lambda e: e.reciprocal(rs_out[:, :N], TMP[0][:, :N]), reads=["TMP0"], writes=[rs_key])

        def stats(src, rkeys, N, rs_out, rs_key):
            for c in range(KC):
                stats_chunk(src, rkeys, N, c)
            stats_finish(N, rs_out, rs_key)

        p.dve(lambda e: e.memset(scr[:, 4:5], EPS), writes=["eps"])

        def modulate_to_U(l, w, s, N):
            for c in range(KC):
                b = c % 2
                p.dve(lambda e, c=c, b=b: e.scalar_tensor_tensor(out=TMP[b][:, :N], in0=H[:, c, :N],
                                                                 scalar=TA[:, l, w, s, c:c + 1], in1=RS[:, :N],
                                                                 op0=ALU.mult, op1=ALU.mult),
                      reads=["H%d" % c, "RS", "TA"], writes=["TMP%d" % b])
                p.act(lambda e, c=c, b=b: e.activation(out=U[:, c, :N], in_=TMP[b][:, :N], func=AF.Identity,
                                                       bias=TS[:, l, w, s, c:c + 1], scale=1.0),
                      reads=["TMP%d" % b, "TS"], writes=["U%d" % c])

        HK = ["H%d" % c for c in range(KC)]
        UK = ["U%d" % c for c in range(KC)]
        YK = ["Y%d" % c for c in range(KC)]

        def residual(l, w, s, N):
            for c in range(KC):
                b = c % 2
                p.dve(lambda e, c=c, b=b: e.scalar_tensor_tensor(out=TMP[b][:, :N], in0=Y[:, c, :N],
                                                                 scalar=TG[:, l, w, s, c:c + 1], in1=RS2[:, :N],
                                                                 op0=ALU.mult, op1=ALU.mult),
                      reads=["Y%d" % c, "RS2", "TG"], writes=["TMP%d" % b])
                p.dve(lambda e, c=c, b=b: e.tensor_tensor(out=H[:, c, :N], in0=H[:, c, :N], in1=TMP[b][:, :N], op=ALU.add),
                      reads=["TMP%d" % b, "H%d" % c], writes=["H%d" % c])

        def ffn(l, w, s, win_b, wout_b, win_keys, wout_keys, N):
            stats(lambda c: H[:, c, :N], lambda c: ["H%d" % c], N, RS, "RS")
            modulate_to_U(l, w, s, N)
            winv = win_b.rearrange("(k p) n -> p k n", p=128)
            woutv = wout_b.rearrange("(f p) n -> p f n", p=128)
            for half in range(2):
                for g in range(11):
                    ff0 = half * 22 + g * 2
                    wb = wictr[0] % 2
                    wictr[0] += 1
                    p.dma("sp", WI[wb][:, :, 0:256], winv[:, :, ff0 * 128:ff0 * 128 + 256], reads=win_keys,
                          writes=["WI%d" % wb])
                    p.dma("sp", WI[wb][:, :, 256:512], winv[:, :, DFF + ff0 * 128:DFF + ff0 * 128 + 256],
                          reads=win_keys, writes=["WI%db" % wb])
                    for j in range(2):
                        pa = PS[(psctr[0] % 2) * 2]
                        pb = PS[(psctr[0] % 2) * 2 + 1]
                        ka = "PS%d" % ((psctr[0] % 2) * 2)
                        kb = "PS%d" % ((psctr[0] % 2) * 2 + 1)
                        psctr[0] += 1
                        for (pt, kk, off) in ((pa, ka, 0), (pb, kb, 256)):
                            for k in range(KC):
                                fl = k in (0, KC - 1)
                                p.pe(lambda e, pt=pt, wb=wb, k=k, off=off, j=j: e.matmul(
                                    pt[:, :N], WI[wb][:, k, off + j * 128:off + (j + 1) * 128], U[:, k, :N],
                                    start=(k == 0), stop=(k == KC - 1)),
                                    reads=["U%d" % k] + ((["WI%d" % wb, "WI%db" % wb] + UK) if fl else []),
                                    writes=[kk] if fl else ())
                        sb = (g * 2 + j) % 2
                        p.act(lambda e, pa=pa, sb=sb: e.activation(out=SA[sb][:, :N], in_=pa[:, :N], func=AF.Silu),
                              reads=[ka], writes=["SA%d" % sb])
                        f = g * 2 + j
                        p.dve(lambda e, pb=pb, sb=sb, f=f: e.tensor_tensor(out=HID[:, f, :N], in0=SA[sb][:, :N],
                                                                           in1=pb[:, :N], op=ALU.mult),
                              reads=[kb, "SA%d" % sb], writes=["HID%d" % f])
                HIDK = ["HID%d" % f for f in range(22)]
                for cg in range(8):
                    wb = woctr[0] % 2
                    woctr[0] += 1
                    p.dma("sp", WO[wb], woutv[:, half * 22:(half + 1) * 22, cg * 256:(cg + 1) * 256], reads=wout_keys,
                          writes=["WO%d" % wb])
                    for j in range(2):
                        c = cg * 2 + j
                        py = PS[4 + c % 2]
                        ky = "PS%d" % (4 + c % 2)
                        for f in range(22):
                            fl = f in (0, 21)
                            p.pe(lambda e, py=py, wb=wb, f=f, j=j: e.matmul(
                                py[:, :N], WO[wb][:, f, j * 128:(j + 1) * 128], HID[:, f, :N],
                                start=(f == 0), stop=(f == 21)),
                                reads=(["WO%d" % wb] + HIDK) if fl else (), writes=[ky] if fl else ())
                        if half == 0:
                            p.act(lambda e, py=py, c=c: e.activation(out=Y[:, c, :N], in_=py[:, :N], func=AF.Copy),
                                  reads=[ky], writes=["Y%d" % c])
                        else:
                            p.dve(lambda e, py=py, c=c: e.tensor_tensor(out=Y[:, c, :N], in0=Y[:, c, :N], in1=py[:, :N],
                                                                        op=ALU.add),
                                  reads=[ky, "Y%d" % c], writes=["Y%d" % c])
                            if c >= 2:
                                stats_chunk(lambda c_: Y[:, c_, :N], lambda c_: ["Y%d" % c_], N, c - 2)
            for c in (KC - 2, KC - 1):
                stats_chunk(lambda c_: Y[:, c_, :N], lambda c_: ["Y%d" % c_], N, c)
            stats_finish(N, RS2, "RS2")
            residual(l, w, s, N)

        def load_x_tile(ti):
            t0, N, w = tiles[ti]
            for tb in range(N // 128):
                xb = XB[tb % 2]
                kx = "Y%d" % (tb % 2)
                src = ctx_d[tb * 128:(tb + 1) * 128, :] if w == 1 else x_d[t0 - NCTX + tb * 128:t0 - NCTX + (tb + 1) * 128, :]
                xkeys = ["Y%d" % c for c in range(4 * (tb % 2), 4 * (tb % 2) + 4)]
                p.dma("sp", xb, src, writes=xkeys)
                for g in range(4):
                    pt = PS[g % 4]
                    kp = "PS%d" % (g % 4)
                    for q in range(4):
                        c = g * 4 + q
                        p.pe(lambda e, pt=pt, xb=xb, c=c, q=q: e.transpose(pt[:, q * 128:(q + 1) * 128],
                                                                          xb[:, c * 128:(c + 1) * 128], identf),
                             reads=xkeys + ["identf"], writes=[kp])
                    hout = H[:, g * 4:(g + 1) * 4, tb * 128:(tb + 1) * 128]
                    hk = ["H%d" % c for c in range(g * 4, g * 4 + 4)]
                    if w == 1:
                        p.dve(lambda e, pt=pt, hout=hout: e.tensor_copy(out=hout, in_=pt[:, :].rearrange("p (q t) -> p q t", t=128)),
                              reads=[kp], writes=hk)
                    else:
                        r0 = (t0 - NCTX + tb * 128) // 64
                        pa = postab[:, 0, :]
                        if g < 2:
                            in1 = bass.AP(pa.tensor, pa.offset + (g * 4) * 64 + r0, [list(pa.ap[0]), [64, 4], [1, 2], [0, 64]])
                        else:
                            in1 = bass.AP(pa.tensor, pa.offset + ((g - 2) * 4) * 64, [list(pa.ap[0]), [64, 4], [0, 2], [1, 64]])
                        ho = hout.rearrange("p q (r t) -> p q r t", t=64)
                        pi = pt[:, :].rearrange("p (q r t) -> p q r t", q=4, t=64)
                        p.dve(lambda e, pi=pi, ho=ho, in1=in1: e.tensor_tensor(out=ho, in0=pi, in1=in1, op=ALU.add),
                              reads=[kp, "postab"], writes=hk)

        def store_out_tile(ti):
            t0, N, w = tiles[ti]
            for tb in range(N // 128):
                ob = XB[tb % 2]
                okeys = ["Y%d" % c for c in range(4 * (tb % 2), 4 * (tb % 2) + 4)]
                for g in range(4):
                    pt = PS[g % 4]
                    kp = "PS%d" % (g % 4)
                    for q in range(4):
                        c = g * 4 + q
                        p.pe(lambda e, pt=pt, c=c, q=q, tb=tb: e.transpose(pt[:, q * 128:(q + 1) * 128],
                                                                          H[:, c, tb * 128:(tb + 1) * 128], identf),
                             reads=["H%d" % c, "identf"], writes=[kp])
                    if g % 2 == 0:
                        p.dve(lambda e, pt=pt, ob=ob, g=g: e.tensor_copy(out=ob[:, g * 512:(g + 1) * 512], in_=pt[:, :]),
                              reads=[kp], writes=[okeys[g]])
                    else:
                        p.act(lambda e, pt=pt, ob=ob, g=g: e.activation(out=ob[:, g * 512:(g + 1) * 512], in_=pt[:, :], func=AF.Copy),
                              reads=[kp], writes=[okeys[g]])
                r0 = t0 - NCTX + tb * 128
                p.dma("pool", out_d[r0:r0 + 128, :], ob, reads=okeys, writes=["out%d" % r0])

        hfmv = hfm_d.rearrange("(c p) t -> p c t", p=128)

        def store_h(ti):
            t0, N, w = tiles[ti]
            p.dma("pool", hfmv[:, :, t0:t0 + N], H[:, :, :N], reads=HK, writes=["hfm%d" % ti])

        def load_h(ti):
            t0, N, w = tiles[ti]
            p.dma("sp", H[:, :, :N], hfmv[:, :, t0:t0 + N], reads=["hfm%d" % ti], writes=HK)


        qT0 = dt_sc("qT0", [512, T], BF16); kT0 = dt_sc("kT0", [512, T], BF16)
        xrT = dt_sc("xrT", [1024, T], F32); grT = dt_sc("grT", [1024, T], BF16)
        lrT = dt_sc("lrT", [32, T], F32)
        v0 = dt_sc("v0", [T, 1024], BF16); sg0 = dt_sc("sg0", [T, 1024], BF16)
        ob0 = dt_sc("ob0", [T, 1024], F32)
        qT1 = dt_sc("qT1", [1024, T], BF16); kT1 = dt_sc("kT1", [1024, T], BF16)
        ktm1 = dt_sc("ktm1", [T, 1024], BF16); v1 = dt_sc("v1", [T, 2048], BF16)
        so1 = dt_sc("so1", [T, 2048], BF16); gT1 = dt_sc("gT1", [32, T], F32)
        hb1 = dt_sc("hb1", [T, 2048], F32)
        ymixv = ymix_d.rearrange("(c p) t -> p c t", p=128)

        STG = [(SA[0], "SA0"), (SA[1], "SA1"), (TMP[0], "TMP0"), (TMP[1], "TMP1")]
        stgc = [0]

        def stg():
            b_ = STG[stgc[0] % 4]
            stgc[0] += 1
            return b_

        cur = {}

        def proj_fm(wb_dram, wk, col0, ncols, evac):
            N = cur["N"]
            wv = wb_dram.rearrange("(k p) n -> p k n", p=128)
            c = col0
            while c < col0 + ncols:
                gw = min(512, col0 + ncols - c)
                wb = wictr[0] % 2
                wictr[0] += 1
                p.dma("sp", WI[wb][:, :, 0:gw], wv[:, :, c:c + gw], reads=wk, writes=["WI%d" % wb, "WI%db" % wb])
                for j0 in range(0, gw, 128):
                    m = min(128, gw - j0)
                    bank = psctr[0] % 4
                    psctr[0] += 1
                    pt = PS[bank]
                    kp = "PS%d" % bank
                    for k in range(KC):
                        fl = k in (0, KC - 1)
                        p.pe(lambda e, pt=pt, wb=wb, k=k, j0=j0, m=m: e.matmul(
                            pt[0:m, :N], WI[wb][:, k, j0:j0 + m], U[:, k, :N], start=(k == 0), stop=(k == KC - 1)),
                            reads=(["WI%d" % wb, "WI%db" % wb] + UK) if fl else (), writes=[kp] if fl else ())
                    evac(pt, kp, m, c + j0)
                c += gw

        def proj_tm(wb_dram, wk, col0, ncols, evac):
            N = cur["N"]
            wv = wb_dram.rearrange("(k p) n -> p k n", p=128)
            for c in range(col0, col0 + ncols, 512):
                wb = wictr[0] % 2
                wictr[0] += 1
                p.dma("sp", WI[wb][:, :, 0:512], wv[:, :, c:c + 512], reads=wk, writes=["WI%d" % wb, "WI%db" % wb])
                for tb in range(N // 128):
                    bank = psctr[0] % 4
                    psctr[0] += 1
                    pt = PS[bank]
                    kp = "PS%d" % bank
                    for k in range(KC):
                        fl = k in (0, KC - 1)
                        p.pe(lambda e, pt=pt, wb=wb, k=k, tb=tb: e.matmul(
                            pt[:, 0:512], U[:, k, tb * 128:(tb + 1) * 128], WI[wb][:, k, 0:512],
                            start=(k == 0), stop=(k == KC - 1)),
                            reads=(["WI%d" % wb, "WI%db" % wb] + UK) if fl else (), writes=[kp] if fl else ())
                    evac(pt, kp, tb, c)

        def ev_fm(dst, colbase, dt, func=None, scale=1.0, bias=None):
            def f(pt, kp, m, col):
                N = cur["N"]; t0 = cur["t0"]
                buf, key = stg()
                sv = buf if dt is F32 else buf.bitcast(BF16)
                if func is not None:
                    if bias is not None:
                        p.act(lambda e: e.activation(out=sv[0:m, :N], in_=pt[0:m, :N], func=func, bias=bias[0:m, :], scale=scale),
                              reads=[kp, "pk"], writes=[key])
                    else:
                        p.act(lambda e: e.activation(out=sv[0:m, :N], in_=pt[0:m, :N], func=func, scale=scale),
                              reads=[kp], writes=[key])
                else:
                    p.dve(lambda e: e.tensor_copy(out=sv[0:m, :N], in_=pt[0:m, :N]), reads=[kp], writes=[key])
                r = col - colbase
                p.dma("pool", dst[r:r + m, t0:t0 + N], sv[0:m, :N], reads=[key], writes=[])
            return f

        def ev_tm(dst, colbase, func=None):
            def f(pt, kp, tb, col):
                t0 = cur["t0"]
                buf, key = stg()
                sv = buf.bitcast(BF16)
                if func is not None:
                    p.act(lambda e: e.activation(out=sv[:, 0:512], in_=pt[:, 0:512], func=func), reads=[kp], writes=[key])
                else:
                    p.dve(lambda e: e.tensor_copy(out=sv[:, 0:512], in_=pt[:, 0:512]), reads=[kp], writes=[key])
                cc_ = col - colbase
                p.dma("pool", dst[t0 + tb * 128:t0 + (tb + 1) * 128, cc_:cc_ + 512], sv[:, 0:512], reads=[key], writes=[])
            return f

        QS = 128.0 ** -0.5

        def inproj0():
            wk = wkeys["abin"]
            proj_fm(abin_b, wk, 0, 512, ev_fm(qT0, 0, BF16, AF.Copy, QS))
            proj_fm(abin_b, wk, 512, 512, ev_fm(kT0, 512, BF16))
            proj_tm(abin_b, wk, 1024, 1024, ev_tm(v0, 1024))
            proj_tm(abin_b, wk, 2048, 1024, ev_tm(sg0, 2048, AF.Silu))
            proj_fm(abin_b, wk, 3072, 32, ev_fm(lrT, 3072, F32))
            proj_fm(abin_b, wk, 3104, 1024, ev_fm(xrT, 3104, F32))
            proj_fm(abin_b, wk, 4128, 1024, ev_fm(grT, 4128, BF16, AF.Gelu_apprx_tanh))

        def inproj1():
            wk = wkeys["mlin"]
            proj_fm(mlin_b, wk, 0, 1024, ev_fm(qT1, 0, BF16, AF.Copy, QS))
            proj_fm(mlin_b, wk, 1024, 1024, ev_fm(kT1, 1024, BF16))
            proj_tm(mlin_b, wk, 1024, 1024, ev_tm(ktm1, 1024))
            proj_tm(mlin_b, wk, 2048, 2048, ev_tm(v1, 2048))
            proj_tm(mlin_b, wk, 4096, 2048, ev_tm(so1, 4096, AF.Sigmoid))
            proj_fm(mlin_b, wk, 6144, 32, ev_fm(gT1, 6144, F32, AF.Identity, 1.0, pkc("mlbg")))

        def premix(l, w, N):
            stats(lambda c: H[:, c, :N], lambda c: ["H%d" % c], N, RS, "RS")
            modulate_to_U(l, w, 1, N)

        def outproj_residual(l, w, wo_b, wk, N, t0):
            p.dma("sp", U[:, :, :N], ymixv[:, :, t0:t0 + N], writes=UK)
            wv = wo_b.rearrange("(k p) n -> p k n", p=128)
            for cg in range(4):
                wb = wictr[0] % 2
                wictr[0] += 1
                p.dma("sp", WI[wb][:, :, 0:512], wv[:, :, cg * 512:(cg + 1) * 512], reads=wk,
                      writes=["WI%d" % wb, "WI%db" % wb])
                for j in range(4):
                    c = cg * 4 + j
                    bank = psctr[0] % 4
                    psctr[0] += 1
                    pt = PS[bank]
                    kp = "PS%d" % bank
                    for k in range(KC):
                        fl = k in (0, KC - 1)
                        p.pe(lambda e, pt=pt, wb=wb, k=k, j=j: e.matmul(
                            pt[:, :N], WI[wb][:, k, j * 128:(j + 1) * 128], U[:, k, :N], start=(k == 0), stop=(k == KC - 1)),
                            reads=(["WI%d" % wb, "WI%db" % wb] + UK) if fl else (), writes=[kp] if fl else ())
                    p.act(lambda e, pt=pt, c=c: e.activation(out=Y[:, c, :N], in_=pt[:, :N], func=AF.Copy),
                          reads=[kp], writes=["Y%d" % c])
            stats(lambda c: Y[:, c, :N], lambda c: ["Y%d" % c], N, RS2, "RS2")
            residual(l, w, 1, N)

        ffn_top = top[0]

        def rev(ap2, n):
            return bass.AP(ap2.tensor, ap2.offset + (n - 1) * ap2.ap[-1][0], [list(ap2.ap[0]), [-ap2.ap[-1][0], n]])

        def bc_chunk(ap2, pos, nch, step=64, inner=64):
            return bass.AP(ap2.tensor, ap2.offset + pos, [list(ap2.ap[0]), [step, nch], [0, inner]])

        def mixer_rg():
            top[0] = base_top
            XR = V(alloc(T), T); XC = V(alloc(T), T); XBb = V(alloc(T // 2), T // 2, BF16)
            A_ = V(alloc(T), T); BX = V(alloc(T), T); HF = V(alloc(T), T); HBk = V(alloc(T), T)
            GG = V(alloc(T // 2), T // 2, BF16); YO = V(alloc(T // 2), T // 2, BF16)
            Rr2 = [V(alloc(512), 512) for _ in range(2)]; Ii2 = [V(alloc(512), 512) for _ in range(2)]; T12 = [V(alloc(512), 512) for _ in range(2)]
            RGW = V(alloc(2048), 4096 // 2, BF16)
            sp8 = V(alloc(16), 16); spt = V(alloc(16), 16)
            p.dma("sp", HF[:, 0:4096], rgw_d[:, :], writes=["HF"])
            p.dve(lambda e: e.tensor_copy(out=RGW, in_=HF[:, 0:4096]), reads=["HF"], writes=["RGW"])
            AK = ["A_%d" % i for i in range(9)]
            BK = ["BX%d" % i for i in range(9)]
            p.act(lambda e: e.activation(out=spt, in_=pkc("rglam"), func=AF.Exp, scale=-1.0), reads=["pk"], writes=["spt"])
            p.act(lambda e: e.activation(out=spt, in_=spt, func=AF.Ln, bias=1.0, scale=1.0), reads=["spt"], writes=["spt"])
            p.dve(lambda e: e.tensor_scalar(sp8, spt, -8.0, None, ALU.mult), reads=["spt"], writes=["sp8"])
            segs = [(0, NCTX), (NCTX, T)]
            def do_cg(cg):
                p.dma("sp", XR, xrT[cg * 128:(cg + 1) * 128, :], writes=["XR"])
                p.dma("sp", GG, grT[cg * 128:(cg + 1) * 128, :], writes=["GG"])
                cw = lambda tap: pkc("convw", tap * 8 + cg, 1)
                cb = pkc("convb", cg, 1)
                for (s0, s1) in segs:
                    p.dve(lambda e, s0=s0, s1=s1: e.tensor_scalar(XC[:, s0:s1], XR[:, s0:s1], cw(2), cb, ALU.mult, ALU.add),
                          reads=["XR", "pk"], writes=["XC"])
                    for tap, off in ((0, -2), (1, -1), (3, 1)):
                        if off < 0:
                            o_ = XC[:, s0 - off:s1]; i_ = XR[:, s0:s1 + off]
                        else:
                            o_ = XC[:, s0:s1 - off]; i_ = XR[:, s0 + off:s1]
                        p.dve(lambda e, o_=o_, i_=i_, tap=tap: e.scalar_tensor_tensor(out=o_, in0=i_, scalar=cw(tap), in1=o_,
                                                                                    op0=ALU.mult, op1=ALU.add),
                              reads=["XR", "XC", "pk"], writes=["XC"])
                p.act(lambda e: e.activation(out=XBb, in_=XC, func=AF.Copy), reads=["XC"], writes=["XBb"])
                for d in range(2):
                    def do_slab(ti, t0, N):
                        bank = psctr[0] % 2
                        psctr[0] += 1
                        Rr = Rr2[bank]; Ii = Ii2[bank]; T1 = T12[bank]
                        kA = "A_%d" % ti; kB = "BX%d" % ti; kR = "Rr%d" % bank; kI = "Ii%d" % bank; kT = "T1_%d" % bank
                        pr = PS[bank * 2]; pi_ = PS[bank * 2 + 1]
                        kr = "PS%d" % (bank * 2); ki_ = "PS%d" % (bank * 2 + 1)
                        wa = RGW[:, ((0 * 2 + d) * 8 + cg) * 128:((0 * 2 + d) * 8 + cg + 1) * 128]
                        wi = RGW[:, ((1 * 2 + d) * 8 + cg) * 128:((1 * 2 + d) * 8 + cg + 1) * 128]
                        p.pe(lambda e, pr=pr, wa=wa, t0=t0, N=N: e.matmul(pr[:, :N], wa, XBb[:, t0:t0 + N], start=True, stop=True),
                             reads=["RGW", "XBb"], writes=[kr])
                        p.pe(lambda e, pi_=pi_, wi=wi, t0=t0, N=N: e.matmul(pi_[:, :N], wi, XBb[:, t0:t0 + N], start=True, stop=True),
                             reads=["RGW", "XBb"], writes=[ki_])
                        ba = pkc("rgba", d * 8 + cg, 1); bi = pkc("rgbi", d * 8 + cg, 1)
                        sc = sp8[:, d * 8 + cg:d * 8 + cg + 1]
                        p.act(lambda e, pr=pr, N=N, ba=ba: e.activation(out=Rr[:, :N], in_=pr[:, :N], func=AF.Sigmoid, bias=ba, scale=1.0),
                              reads=[kr, "pk"], writes=[kR])
                        p.act(lambda e, pi_=pi_, N=N, bi=bi: e.activation(out=Ii[:, :N], in_=pi_[:, :N], func=AF.Sigmoid, bias=bi, scale=1.0),
                              reads=[ki_, "pk"], writes=[kI])
                        p.act(lambda e, t0=t0, N=N, sc=sc: e.activation(out=A_[:, t0:t0 + N], in_=Rr[:, :N], func=AF.Exp, scale=sc),
                              reads=[kR, "sp8"], writes=[kA])
                        p.dve(lambda e, t0=t0, N=N: e.tensor_tensor(out=T1[:, :N], in0=A_[:, t0:t0 + N], in1=A_[:, t0:t0 + N], op=ALU.mult),
                              reads=[kA], writes=[kT])
                        p.act(lambda e, N=N: e.activation(out=T1[:, :N], in_=T1[:, :N], func=AF.Sqrt, bias=1.0, scale=-1.0),
                              reads=[kT], writes=[kT])
                        p.dve(lambda e, t0=t0, N=N: e.tensor_tensor(out=Ii[:, :N], in0=Ii[:, :N], in1=XC[:, t0:t0 + N], op=ALU.mult),
                              reads=[kI, "XC"], writes=[kI])
                        p.dve(lambda e, t0=t0, N=N: e.tensor_tensor(out=BX[:, t0:t0 + N], in0=Ii[:, :N], in1=T1[:, :N], op=ALU.mult),
                              reads=[kI, kT], writes=[kB])
                    for ti_, (t0_, N_, w_) in enumerate(tiles):
                        do_slab(ti_, t0_, N_)
                    if d == 0:
                        p.dve(lambda e: e.tensor_tensor_scan(HF, A_, BX, 0.0, ALU.mult, ALU.add), reads=AK + BK, writes=["HF"])
                    else:
                        p.dve(lambda e: e.tensor_tensor_scan(rev(HBk[:, 0:NCTX], NCTX), rev(A_[:, 0:NCTX], NCTX), rev(BX[:, 0:NCTX], NCTX),
                                                             0.0, ALU.mult, ALU.add), reads=AK + BK, writes=["HBk"])
                        p.dve(lambda e: e.tensor_tensor_scan(rev(HBk[:, NCTX:T], NLAT), rev(A_[:, NCTX:T], NLAT), rev(BX[:, NCTX:T], NLAT),
                                                             HBk[:, 0:1], ALU.mult, ALU.add), reads=AK + BK + ["HBk"], writes=["HBk"])
                p.dve(lambda e: e.tensor_tensor(out=HF, in0=HF, in1=HBk, op=ALU.add), reads=["HF", "HBk"], writes=["HF"])
                p.dve(lambda e: e.tensor_tensor(out=YO, in0=HF, in1=GG, op=ALU.mult), reads=["HF", "GG"], writes=["YO"])
                p.dma("pool", ymix_d[1024 + cg * 128:1024 + (cg + 1) * 128, :], YO, reads=["YO"], writes=[])

            for cg in range(8):
                do_cg(cg)

        def mixer_chunk(layer):
            top[0] = base_top
            NH = 4 if layer == 0 else 8
            PSY = PS[0][:, :].bitcast(BF16)
            DV = 256
            VW = NH * DV
            VA = DV + 1 if layer == 1 else DV
            qT_d, kT_d = (qT0, kT0) if layer == 0 else (qT1, kT1)
            v_d = v0 if layer == 0 else v1
            gate_d = sg0 if layer == 0 else so1
            ob_d = ob0 if layer == 0 else hb1
            MF = V(alloc(64), 64); MB = V(alloc(64), 64)
            RM = V(alloc(512), 512)
            GN = V(alloc(VW), VW)
            Qs = [V(alloc(256), 256, BF16) for _ in range(NH)]
            Ks = [V(alloc(256), 256, BF16) for _ in range(NH)]
            Vt = V(alloc(8 * NH * VA // 2 + 8), 8 * NH * VA // 2 + 8, BF16)[:, 0:8 * NH * VA].rearrange("p (c h v) -> p c h v", c=8, h=NH)
            S32 = V(alloc(NH * VA), NH * VA, F32, "p (h v) -> p h v", h=NH)
            Sb2 = [V(alloc(NH * VA // 2 + 4), NH * VA // 2 + 4, BF16)[:, 0:NH * VA].rearrange("p (h v) -> p h v", h=NH) for _ in range(2)]
            STM = [V(alloc(NH * 32), NH * 32, BF16, "p (h i) -> p h i", h=NH) for _ in range(2)]
            NB = 2 if layer == 0 else 1
            OT = [V(alloc(VW), VW) for _ in range(2)]
            OBc = [V(alloc(VW), VW) for _ in range(NB)] * (3 - NB)
            SGc = [V(alloc(VW // 2), VW // 2, BF16) for _ in range(NB)] * (3 - NB)
            T1 = V(alloc(VW), VW)
            YG = [V(alloc(VW // 2), VW // 2, BF16) for _ in range(NB)] * (3 - NB)
            JK = V(alloc(128), 128, BF16)
            SS = V(alloc(NH), NH); RSg = V(alloc(NH), NH)
            YM = V(alloc(VW // 128 * 256), VW // 128 * 256, BF16, "p (c n) -> p c n", n=512)
            p.pool(lambda e: e.memset(MF, 1.0), writes=["MF"])
            p.pool(lambda e: e.affine_select(out=MF[0:64, :], in_=MF[0:64, :], pattern=[[1, 64]], compare_op=ALU.is_ge, fill=0.0,
                                             base=0, channel_multiplier=-1), reads=["MF"], writes=["MF"])
            p.pool(lambda e: e.memset(MB, 1.0), writes=["MB"])
            p.pool(lambda e: e.affine_select(out=MB[0:64, :], in_=MB[0:64, :], pattern=[[-1, 64]], compare_op=ALU.is_ge, fill=0.0,
                                             base=0, channel_multiplier=1), reads=["MB"], writes=["MB"])
            p.dma("sp", GN, (gn4_d if layer == 0 else mn8_d)[:, :], writes=["GN"])
            if layer == 0:
                WA2 = V(alloc(1024), 1024)
                LR = V(alloc(512), 512)
                p.dma("sp", WA2[0:16, :], wa2_d[:, :], writes=["WA2"])
                p.dve(lambda e: e.memset(RM, 1.0), writes=["RM"])
                p.dve(lambda e: e.memset(RM.rearrange("p (c j) -> p c j", j=64)[:, :, 0:1], 0.0), reads=["RM"], writes=["RM"])
                ZB = [V(alloc(512), 512) for _ in range(NH)]
                BB = [V(alloc(512), 512) for _ in range(NH)]
                D1 = [V(alloc(512), 512) for _ in range(2)]
                EE = [V(alloc(512), 512) for _ in range(2)]
                E3 = [V(alloc(512), 512) for _ in range(NH)]
                QSs = [V(alloc(256), 256, BF16) for _ in range(NH)]
                KSs = [V(alloc(256), 256, BF16) for _ in range(NH)]
                QEs = [V(alloc(256), 256, BF16) for _ in range(NH)]
                KDs = [V(alloc(256), 256, BF16) for _ in range(NH)]
                KDT = [V(alloc(256), 256, BF16) for _ in range(2)]
            else:
                p.dve(lambda e: e.memset(RM, 1.0), writes=["RM"])
                p.dve(lambda e: e.memset(RM.rearrange("p (c j) -> p c j", j=64)[:, :, 0:1], 0.0), reads=["RM"], writes=["RM"])
                RMN = V(alloc(512), 512)
                p.dve(lambda e: e.memset(RMN, 0.0), writes=["RMN"])
                p.dve(lambda e: e.memset(RMN.rearrange("p (c j) -> p c j", j=64)[:, :, 0:1], -1e30), reads=["RMN"], writes=["RMN"])
                I8 = V(alloc(512), 512); F8 = V(alloc(512), 512); B8 = V(alloc(512), 512); GK = V(alloc(512), 512)
                PM = V(alloc(512), 512); WK8 = V(alloc(512), 512); EB8 = V(alloc(512), 512)
                MN = V(alloc(8), 8); MC = V(alloc(8), 8); RC = V(alloc(8), 8); DC = V(alloc(8), 8)
                MCAR = V(alloc(1), 1)
                DEXP = V(alloc(64), 64)
                ONES32 = V(alloc(128), 128)
                WKT = V(alloc(64), 64); EBT = V(alloc(64), 64)
                DECB = V(alloc(64), 64, F32, "p (h c) -> p h c", h=8)
                KTMc = [V(alloc(512), 512, BF16) for _ in range(2)]
                KTS = [V(alloc(512), 512, BF16) for _ in range(2)]
                TMPS = [V(alloc(512), 512) for _ in range(2)]
                DEN = V(alloc(8), 8); RDN = V(alloc(8), 8)
                p.dve(lambda e: e.memset(ONES32, 1.0), writes=["ONES32"])
                p.dve(lambda e: e.memset(Vt[:, :, :, 256:257], 1.0), writes=["Vt"])

            def do_dir(d):
                fwd = d == 0
                mask = MF if fwd else MB
                mk = "MF" if fwd else "MB"
                p.dve(lambda e: e.memset(S32, 0.0), writes=["S32"])
                p.dve(lambda e: e.memset(Sb2[0], 0.0), writes=["Sb_0"] + ["Sb%d_0" % h for h in range(NH)])
                p.dve(lambda e: e.memset(Sb2[1], 0.0), writes=["Sb_1"] + ["Sb%d_1" % h for h in range(NH)])
                if layer == 1:
                    p.dve(lambda e: e.memset(MCAR[0:8, :], 0.0), writes=["MCAR"])
                order = list(range(9)) if fwd else [0] + list(range(8, 0, -1))
                def slab(ti):
                    t0, N, w = tiles[ti]
                    nch = N // 64
                    readout = not (layer == 1 and w == 1)
                    for h in range(NH):
                        p.dma("sp", Qs[h][:, :N], qT_d[h * 128:(h + 1) * 128, t0:t0 + N], writes=["Q%d" % h])
                        p.dma("sp", Ks[h][:, :N], kT_d[h * 128:(h + 1) * 128, t0:t0 + N], writes=["K%d" % h])
                    vsrc = v_d[t0:t0 + N, :].rearrange("(c j) (h v) -> j c h v", j=64, v=DV)
                    if layer == 0:
                        p.dma("sp", Vt[0:64, 0:nch, 0:4, 0:DV], vsrc[:, :, 0:4, :], writes=["Vt"])
                    else:
                        for c_ in range(nch):
                            p.dma("sp", Vt[0:64, c_, :, 0:DV], vsrc[:, c_, :, :], writes=["Vt"])
                    first_pos = 0 if fwd else 63
                    last_pos = 63 if fwd else 0
                    if layer == 0:
                        p.dma("sp", LR[0:16, :N], lrT[d * 16:(d + 1) * 16, t0:t0 + N], writes=["LR"])
                        for h in range(NH):
                            pz = PS[h]
                            kz = "PS%d" % h
                            p.pe(lambda e, pz=pz, h=h: e.matmul(pz[:, :N], WA2[0:16, d * 512 + h * 128:d * 512 + (h + 1) * 128], LR[0:16, :N],
                                                                start=True, stop=True), reads=["WA2", "LR"], writes=[kz])
                            bal = pkc("balpha", d * 4 + h, 1)
                            p.dve(lambda e, pz=pz, h=h, bal=bal: e.tensor_scalar(ZB[h][:, :N], pz[:, :N], bal, -20.0, ALU.add, ALU.max),
                                  reads=[kz, "pk"], writes=["ZB%d" % h])
                            p.act(lambda e, h=h: e.activation(out=ZB[h][:, :N], in_=ZB[h][:, :N], func=AF.Exp, scale=-1.0),
                                  reads=["ZB%d" % h], writes=["ZB%d" % h])
                            p.act(lambda e, h=h: e.activation(out=ZB[h][:, :N], in_=ZB[h][:, :N], func=AF.Ln, bias=1.0, scale=1.0),
                                  reads=["ZB%d" % h], writes=["ZB%d" % h])
                            p.dve(lambda e, h=h: e.tensor_scalar(ZB[h][:, :N], ZB[h][:, :N], -1.0 / 16.0, -1.0, ALU.mult, ALU.max),
                                  reads=["ZB%d" % h], writes=["ZB%d" % h])
                            if fwd:
                                p.dve(lambda e, h=h: e.tensor_tensor_scan(BB[h][:, :N], RM[:, :N], ZB[h][:, :N], 0.0, ALU.mult, ALU.add),
                                      reads=["ZB%d" % h, "RM"], writes=["BB%d" % h])
                            else:
                                p.dve(lambda e, h=h: e.tensor_tensor_scan(rev(BB[h][:, :N], N), RM[:, :N], rev(ZB[h][:, :N], N), 0.0,
                                                                          ALU.mult, ALU.add),
                                      reads=["ZB%d" % h, "RM"], writes=["BB%d" % h])
                            ref_pos = 32 if fwd else 31
                            b3 = BB[h][:, :N].rearrange("p (c j) -> p c j", j=64)
                            dd = D1[h % 2]; dk_ = "D1%d" % (h % 2)
                            ea = EE[0]; eb_ = EE[1]
                            p.dve(lambda e, h=h, b3=b3, dd=dd: e.tensor_tensor(out=dd[:, :N].rearrange("p (c j) -> p c j", j=64), in0=b3,
                                                                               in1=bc_chunk(BB[h], ref_pos, nch), op=ALU.subtract),
                                  reads=["BB%d" % h], writes=[dk_])
                            p.act(lambda e, dd=dd, ea=ea: e.activation(out=ea[:, :N], in_=dd[:, :N], func=AF.Exp), reads=[dk_], writes=["EE0"])
                            p.pool(lambda e, h=h, ea=ea: e.tensor_tensor(out=QSs[h][:, :N], in0=Qs[h][:, :N], in1=ea[:, :N], op=ALU.mult),
                                  reads=["EE0", "Q%d" % h], writes=["QS%d" % h])
                            p.act(lambda e, dd=dd, eb_=eb_: e.activation(out=eb_[:, :N], in_=dd[:, :N], func=AF.Exp, scale=-1.0),
                                  reads=[dk_], writes=["EE1"])
                            p.pool(lambda e, h=h, eb_=eb_: e.tensor_tensor(out=KSs[h][:, :N], in0=Ks[h][:, :N], in1=eb_[:, :N], op=ALU.mult),
                                  reads=["EE1", "K%d" % h], writes=["KS%d" % h])
                            p.act(lambda e, h=h: e.activation(out=E3[h][:, :N], in_=BB[h][:, :N], func=AF.Exp), reads=["BB%d" % h],
                                  writes=["E3%d" % h])
                            p.pool(lambda e, h=h: e.tensor_tensor(out=QEs[h][:, :N], in0=Qs[h][:, :N], in1=E3[h][:, :N], op=ALU.mult),
                                  reads=["E3%d" % h, "Q%d" % h], writes=["QE%d" % h])
                            p.dve(lambda e, h=h, b3=b3, dd=dd: e.tensor_tensor(out=dd[:, :N].rearrange("p (c j) -> p c j", j=64),
                                                                               in0=bc_chunk(BB[h], last_pos, nch), in1=b3, op=ALU.subtract),
                                  reads=["BB%d" % h], writes=[dk_])
                            p.act(lambda e, dd=dd, ea=ea: e.activation(out=ea[:, :N], in_=dd[:, :N], func=AF.Exp), reads=[dk_], writes=["EE0"])
                            p.pool(lambda e, h=h, ea=ea: e.tensor_tensor(out=KDs[h][:, :N], in0=Ks[h][:, :N], in1=ea[:, :N], op=ALU.mult),
                                  reads=["EE0", "K%d" % h], writes=["KD%d" % h])
                    else:
                        p.dma("sp", I8[0:8, :N], gT1[d * 16:d * 16 + 8, t0:t0 + N], writes=["I8"])
                        p.dma("sp", F8[0:8, :N], gT1[d * 16 + 8:d * 16 + 16, t0:t0 + N], writes=["F8"])
                        p.act(lambda e: e.activation(out=F8[0:8, :N], in_=F8[0:8, :N], func=AF.Exp, scale=-1.0), reads=["F8"], writes=["F8"])
                        p.act(lambda e: e.activation(out=F8[0:8, :N], in_=F8[0:8, :N], func=AF.Ln, bias=1.0, scale=1.0), reads=["F8"], writes=["F8"])
                        p.dve(lambda e: e.tensor_scalar(F8[0:8, :N], F8[0:8, :N], -1.0, None, ALU.mult), reads=["F8"], writes=["F8"])
                        if fwd:
                            p.dve(lambda e: e.tensor_tensor_scan(B8[0:8, :N], RM[0:8, :N], F8[0:8, :N], 0.0, ALU.mult, ALU.add),
                                  reads=["F8", "RM"], writes=["B8"])
                        else:
                            p.dve(lambda e: e.tensor_tensor_scan(rev(B8[0:8, :N], N), RM[0:8, :N], rev(F8[0:8, :N], N), 0.0, ALU.mult, ALU.add),
                                  reads=["F8", "RM"], writes=["B8"])
                        p.dve(lambda e: e.tensor_tensor(out=GK[0:8, :N], in0=I8[0:8, :N], in1=B8[0:8, :N], op=ALU.subtract),
                              reads=["I8", "B8"], writes=["GK"])
                        if fwd:
                            p.dve(lambda e: e.tensor_tensor_scan(PM[0:8, :N], RMN[0:8, :N], GK[0:8, :N], -1e30, ALU.add, ALU.max),
                                  reads=["GK", "RMN"], writes=["PM"])
                        else:
                            p.dve(lambda e: e.tensor_tensor_scan(rev(PM[0:8, :N], N), RMN[0:8, :N], rev(GK[0:8, :N], N), -1e30, ALU.add, ALU.max),
                                  reads=["GK", "RMN"], writes=["PM"])
                        def strided(ap2, pos):
                            return bass.AP(ap2.tensor, ap2.offset + pos, [[ap2.ap[0][0], 8], [64, nch]])
                        pml = strided(PM, last_pos); bl = strided(B8, last_pos)
                        mn = MN[0:8, 0:nch]; mc = MC[0:8, 0:nch]
                        if fwd:
                            p.dve(lambda e: e.tensor_tensor_scan(mn, pml, bl, MCAR[0:8, :], ALU.max, ALU.add),
                                  reads=["PM", "B8", "MCAR"], writes=["MN"])
                            p.dve(lambda e: e.tensor_copy(out=MC[0:8, 0:1], in_=MCAR[0:8, :]), reads=["MCAR"], writes=["MC"])
                            if nch > 1:
                                p.dve(lambda e: e.tensor_copy(out=MC[0:8, 1:nch], in_=MN[0:8, 0:nch - 1]), reads=["MN", "MC"], writes=["MC"])
                            p.dve(lambda e: e.tensor_copy(out=MCAR[0:8, :], in_=MN[0:8, nch - 1:nch]), reads=["MN", "MC"], writes=["MCAR"])
                        else:
                            p.dve(lambda e: e.tensor_tensor_scan(rev(mn, nch), rev(pml, nch), rev(bl, nch), MCAR[0:8, :], ALU.max, ALU.add),
                                  reads=["PM", "B8", "MCAR"], writes=["MN"])
                            p.dve(lambda e: e.tensor_copy(out=MC[0:8, nch - 1:nch], in_=MCAR[0:8, :]), reads=["MCAR"], writes=["MC"])
                            if nch > 1:
                                p.dve(lambda e: e.tensor_copy(out=MC[0:8, 0:nch - 1], in_=MN[0:8, 1:nch]), reads=["MN", "MC"], writes=["MC"])
                            p.dve(lambda e: e.tensor_copy(out=MCAR[0:8, :], in_=MN[0:8, 0:1]), reads=["MN", "MC"], writes=["MCAR"])
                        p.dve(lambda e: e.tensor_tensor(out=RC[0:8, 0:nch], in0=mn, in1=bl, op=ALU.subtract), reads=["MN", "B8"], writes=["RC"])
                        p.dve(lambda e: e.tensor_tensor(out=DC[0:8, 0:nch], in0=mc, in1=RC[0:8, 0:nch], op=ALU.subtract),
                              reads=["MC", "RC"], writes=["DC"])
                        p.act(lambda e: e.activation(out=DC[0:8, 0:nch], in_=DC[0:8, 0:nch], func=AF.Exp), reads=["DC"], writes=["DC"])
                        rcb = bass.AP(RC.tensor, RC.offset, [[RC.ap[0][0], 8], [1, nch], [0, 64]])
                        p.dve(lambda e: e.tensor_tensor(out=WK8[0:8, :N].rearrange("p (c j) -> p c j", j=64),
                                                        in0=GK[0:8, :N].rearrange("p (c j) -> p c j", j=64), in1=rcb, op=ALU.subtract),
                              reads=["GK", "RC"], writes=["WK8"])
                        p.act(lambda e: e.activation(out=WK8[0:8, :N], in_=WK8[0:8, :N], func=AF.Exp), reads=["WK8"], writes=["WK8"])
                        p.dve(lambda e: e.tensor_tensor(out=EB8[0:8, :N].rearrange("p (c j) -> p c j", j=64),
                                                        in0=B8[0:8, :N].rearrange("p (c j) -> p c j", j=64), in1=rcb, op=ALU.add),
                              reads=["B8", "RC"], writes=["EB8"])
                        p.act(lambda e: e.activation(out=EB8[0:8, :N], in_=EB8[0:8, :N], func=AF.Exp, scale=-1.0), reads=["EB8"], writes=["EB8"])
                        ptw = PS[0]; pte = PS[1]; pdc = PS[2]
                        for c in range(nch):
                            p.pe(lambda e, c=c: e.transpose(ptw[0:64, c * 8:(c + 1) * 8], WK8[0:8, c * 64:(c + 1) * 64], identf[0:8, 0:8]),
                                 reads=["WK8", "identf"], writes=["PS0"])
                            p.pe(lambda e, c=c: e.transpose(pte[0:64, c * 8:(c + 1) * 8], EB8[0:8, c * 64:(c + 1) * 64], identf[0:8, 0:8]),
                                 reads=["EB8", "identf"], writes=["PS1"])
                        p.dve(lambda e: e.tensor_copy(out=WKT[0:64, 0:nch * 8], in_=ptw[0:64, 0:nch * 8]), reads=["PS0"], writes=["WKT"])
                        p.dve(lambda e: e.tensor_copy(out=EBT[0:64, 0:nch * 8], in_=pte[0:64, 0:nch * 8]), reads=["PS1"], writes=["EBT"])
                        dcb = bass.AP(DC.tensor, DC.offset, [[DC.ap[0][0], 8], [0, 8], [1, nch]])
                        idb = bass.AP(identf.tensor, identf.offset, [[identf.ap[0][0], 8], [1, 8], [0, nch]])
                        p.dve(lambda e: e.tensor_tensor(out=DEXP[0:8, 0:8 * nch].rearrange("p (h c) -> p h c", h=8), in0=dcb, in1=idb, op=ALU.mult),
                              reads=["DC", "identf"], writes=["DEXP"])
                        p.pe(lambda e: e.matmul(pdc[:, 0:8 * nch], ONES32[0:8, :], DEXP[0:8, 0:8 * nch], start=True, stop=True),
                             reads=["ONES32", "DEXP"], writes=["PS2"])
                        p.dve(lambda e: e.tensor_copy(out=DECB[:, :, 0:nch], in_=pdc[:, 0:8 * nch].rearrange("p (h c) -> p h c", h=8)),
                              reads=["PS2"], writes=["DECB"])

                    corder = range(nch) if fwd else range(nch - 1, -1, -1)
                    def chunk(ci, c):
                        cs = slice(c * 64, (c + 1) * 64)
                        tg = t0 + c * 64
                        sb_ = ci % 2
                        stm = STM[sb_]
                        Sb = Sb2[sb_]
                        Sbn = Sb2[1 - sb_]
                        kst = "STM%d" % sb_
                        pst = PS[4]
                        for h in range(NH):
                            lq = QSs[h] if layer == 0 else Qs[h]
                            lk = KSs[h] if layer == 0 else Ks[h]
                            rk_ = ["QS%d" % h, "KS%d" % h] if layer == 0 else ["Q%d" % h, "K%d" % h]
                            p.pe(lambda e, h=h, lq=lq, lk=lk, cs=cs: e.matmul(pst[0:64, h * 64:(h + 1) * 64], lk[:, cs], lq[:, cs],
                                                                              start=True, stop=True), reads=rk_, writes=["PS4"])
                        pv = pst[0:64, 0:NH * 64].rearrange("p (h i) -> p h i", h=NH)
                        mb_ = bass.AP(mask.tensor, mask.offset, [[mask.ap[0][0], 64], [0, NH], [1, 64]])
                        if layer == 0:
                            p.dve(lambda e, stm=stm, pv=pv, mb_=mb_: e.tensor_tensor(out=stm[0:64, :, :], in0=pv, in1=mb_, op=ALU.mult),
                                  reads=["PS4", mk], writes=[kst])
                        else:
                            wkb = bass.AP(WKT.tensor, WKT.offset + c * 8, [[WKT.ap[0][0], 64], [1, NH], [0, 64]])
                            tmps = TMPS[sb_][0:64, :].rearrange("p (h i) -> p h i", h=NH)
                            p.dve(lambda e, pv=pv, wkb=wkb, tmps=tmps: e.tensor_tensor(out=tmps, in0=pv, in1=wkb, op=ALU.mult),
                                  reads=["PS4", "WKT"], writes=["TMPS%d" % sb_])
                            p.pool(lambda e, stm=stm, tmps=tmps, mb_=mb_: e.tensor_tensor(out=stm[0:64, :, :], in0=tmps, in1=mb_, op=ALU.mult),
                                  reads=["TMPS%d" % sb_, mk], writes=[kst])
                        if layer == 0:
                            kdt = KDT[sb_]
                            for h in range(NH):
                                p.pe(lambda e, h=h, cs=cs: e.transpose(PSB[0:64, h * 128:(h + 1) * 128], KDs[h][:, cs], identb),
                                     reads=["KD%d" % h, "identb"], writes=["PSBa"])
                            p.act(lambda e, kdt=kdt: e.activation(out=kdt[0:64, 0:512], in_=PSB[0:64, 0:512], func=AF.Copy),
                                  reads=["PSBa"], writes=["KDT%d" % sb_])
                        else:
                            ktm = KTMc[sb_]
                            kts = KTS[sb_]
                            p.dma("sp", ktm[0:64, :], ktm1[tg:tg + 64, :], writes=["KTM%d" % sb_])
                            wkk = bass.AP(WKT.tensor, WKT.offset + c * 8, [[WKT.ap[0][0], 64], [1, NH], [0, 128]])
                            p.pool(lambda e, ktm=ktm, kts=kts, wkk=wkk: e.tensor_tensor(
                                out=kts[0:64, :].rearrange("p (h d) -> p h d", h=NH), in0=ktm[0:64, :].rearrange("p (h d) -> p h d", h=NH),
                                in1=wkk, op=ALU.mult), reads=["KTM%d" % sb_, "WKT"], writes=["KTS%d" % sb_])
                            for h in range(NH):
                                p.act(lambda e, h=h, c=c, Sb=Sb: e.activation(out=Sb[:, h, :], in_=S32[:, h, :], func=AF.Copy, scale=DECB[:, h, c:c + 1]),
                                      reads=["S32_%d" % h, "S32", "DECB"], writes=["Sb%d_%d" % (h, sb_)])
                        if layer == 0:
                            for h in range(NH):
                                pu = PS[2 + h // 2]
                                kpu = "PS%d" % (2 + h // 2)
                                usl = pu[:, (h % 2) * 256:(h % 2 + 1) * 256]
                                p.pe(lambda e, h=h, usl=usl, kdt=kdt, c=c: e.matmul(usl, kdt[0:64, h * 128:(h + 1) * 128], Vt[0:64, c, h, :], start=True, stop=True),
                                     reads=["KDT%d" % sb_, "Vt"], writes=[kpu])
                                dsc = E3[h][:, c * 64 + last_pos:c * 64 + last_pos + 1]
                                p.dve(lambda e, h=h, usl=usl, dsc=dsc: e.scalar_tensor_tensor(out=S32[:, h, :], in0=S32[:, h, :], scalar=dsc, in1=usl,
                                                                                             op0=ALU.mult, op1=ALU.add),
                                      reads=[kpu, "E3%d" % h, "S32"], writes=["S32_%d" % h])
                            p.act(lambda e, Sbn=Sbn: e.activation(out=Sbn, in_=S32, func=AF.Copy), reads=["S32_%d" % h for h in range(NH)] + ["S32"],
                                  writes=["Sb_%d" % (1 - sb_)])
                        else:
                            for h in range(NH):
                                pu = PS[2 + h % 2]
                                kpu = "PS%d" % (2 + h % 2)
                                usl = pu[:, 0:VA]
                                p.pe(lambda e, h=h, usl=usl, kts=kts, c=c: e.matmul(usl, kts[0:64, h * 128:(h + 1) * 128], Vt[0:64, c, h, :], start=True, stop=True),
                                     reads=["KTS%d" % sb_, "Vt"], writes=[kpu])
                                p.dve(lambda e, h=h, usl=usl, c=c: e.scalar_tensor_tensor(out=S32[:, h, :], in0=S32[:, h, :], scalar=DECB[:, h, c:c + 1], in1=usl,
                                                                                          op0=ALU.mult, op1=ALU.add),
                                      reads=[kpu, "DECB", "S32", "Sb%d_%d" % (h, sb_)], writes=["S32_%d" % h])
                        post = None
                        if readout:
                            ot = OT[sb_]
                            kot = "OT%d" % sb_
                            if layer == 0:
                                for h in range(NH):
                                    po = PS[5 + h // 2]
                                    kpo = "PS%d" % (5 + h // 2)
                                    osl = po[0:64, (h % 2) * 256:(h % 2 + 1) * 256]
                                    p.pe(lambda e, h=h, osl=osl, stm=stm, c=c: e.matmul(osl, stm[0:64, h, :], Vt[0:64, c, h, :], start=True, stop=False),
                                         reads=[kst, "Vt"], writes=[kpo])
                                    p.pe(lambda e, h=h, osl=osl, cs=cs, Sb=Sb: e.matmul(osl, QEs[h][:, cs], Sb[:, h, :], start=False, stop=True),
                                         reads=["QE%d" % h, "Sb_%d" % sb_], writes=[kpo])
                                if not fwd:
                                    p.act(lambda e, ot=ot: e.activation(out=ot[0:64, 0:512], in_=PS[5][0:64, :], func=AF.Copy), reads=["PS5"], writes=[kot])
                                    p.dve(lambda e, ot=ot: e.tensor_copy(out=ot[0:64, 512:1024], in_=PS[6][0:64, :]), reads=["PS6"], writes=[kot + "b"])
                                else:
                                    obc = OBc[sb_]
                                    p.dma("sp", obc[0:64, :], ob_d[tg:tg + 64, :], reads=["sc_ob%d" % tg], writes=["OBc%d" % (sb_ % NB)])
                                    p.dve(lambda e, ot=ot, obc=obc: e.tensor_tensor(out=ot[0:64, 0:512], in0=PS[5][0:64, :], in1=obc[0:64, 0:512], op=ALU.add),
                                          reads=["PS5", "OBc%d" % (sb_ % NB)], writes=[kot])
                                    p.dve(lambda e, ot=ot, obc=obc: e.tensor_tensor(out=ot[0:64, 512:1024], in0=PS[6][0:64, :], in1=obc[0:64, 512:1024], op=ALU.add),
                                          reads=["PS6", "OBc%d" % (sb_ % NB)], writes=[kot + "b"])
                            else:
                                if fwd:
                                    obc = OBc[sb_]
                                    p.dma("sp", obc[0:64, :], ob_d[tg:tg + 64, :], reads=["sc_ob%d" % tg], writes=["OBc%d" % (sb_ % NB)])
                                for h in range(NH):
                                    po = PS[5 + h % 2]
                                    kpo = "PS%d" % (5 + h % 2)
                                    osl = po[0:64, 0:VA]
                                    p.pe(lambda e, h=h, osl=osl, stm=stm, c=c: e.matmul(osl, stm[0:64, h, :], Vt[0:64, c, h, :], start=True, stop=False),
                                         reads=[kst, "Vt"], writes=[kpo])
                                    p.pe(lambda e, h=h, osl=osl, cs=cs, Sb=Sb: e.matmul(osl, Qs[h][:, cs], Sb[:, h, :], start=False, stop=True),
                                         reads=["Q%d" % h, "Sb%d_%d" % (h, sb_)], writes=[kpo])
                                    p.dve(lambda e, h=h, po=po: e.tensor_scalar(DEN[0:64, h:h + 1], po[0:64, 256:257], -1.0, None, ALU.mult),
                                          reads=[kpo], writes=["DEN%d" % h])
                                    p.dve(lambda e, h=h, po=po, c=c: e.scalar_tensor_tensor(
                                        out=DEN[0:64, h:h + 1], in0=DEN[0:64, h:h + 1], scalar=EBT[0:64, c * 8 + h:c * 8 + h + 1],
                                        in1=po[0:64, 256:257], op0=ALU.max, op1=ALU.max),
                                          reads=[kpo, "EBT", "DEN%d" % h], writes=["DEN%d" % h])
                                    p.dve(lambda e, h=h: e.reciprocal(RDN[0:64, h:h + 1], DEN[0:64, h:h + 1]), reads=["DEN%d" % h], writes=["RDN%d" % h])
                                    if fwd:
                                        p.dve(lambda e, h=h, po=po, ot=ot, obc=obc: e.scalar_tensor_tensor(
                                            out=ot[0:64, h * 256:(h + 1) * 256], in0=po[0:64, 0:256], scalar=RDN[0:64, h:h + 1],
                                            in1=obc[0:64, h * 256:(h + 1) * 256], op0=ALU.mult, op1=ALU.add),
                                            reads=[kpo, "RDN%d" % h, "OBc%d" % (sb_ % NB)], writes=[kot + "_%d" % h])
                                    else:
                                        p.act(lambda e, h=h, po=po, ot=ot: e.activation(out=ot[0:64, h * 256:(h + 1) * 256], in_=po[0:64, 0:256],
                                                                                        func=AF.Copy, scale=RDN[0:64, h:h + 1]),
                                              reads=[kpo, "RDN%d" % h], writes=[kot + "_%d" % h])
                            okeys = [kot, kot + "b"] if layer == 0 else [kot + "_%d" % h for h in range(NH)]
                            def post():
                                if not fwd:
                                    p.dma("pool", ob_d[tg:tg + 64, :], ot[0:64, :], reads=okeys, writes=["sc_ob%d" % tg])
                                else:
                                    sgc = SGc[sb_]
                                    p.dma("sp", sgc[0:64, :], gate_d[tg:tg + 64, :], writes=["SGc%d" % (sb_ % NB)])
                                    for h in range(NH):
                                        p.act(lambda e, h=h, ot=ot: e.activation(out=JK[0:64, 0:256], in_=ot[0:64, h * 256:(h + 1) * 256], func=AF.Square,
                                                                                 accum_out=SS[0:64, h:h + 1]), reads=okeys, writes=["JK", "SS%d" % h])
                                    p.act(lambda e: e.activation(out=SS[0:64, :], in_=SS[0:64, :], func=AF.Sqrt, bias=scr[0:64, 4:5], scale=1.0 / DV),
                                          reads=["SS%d" % h for h in range(NH)] + ["eps"], writes=["SSq"])
                                    p.dve(lambda e: e.reciprocal(RSg[0:64, :], SS[0:64, :]), reads=["SSq"], writes=["RSg"] + ["SS%d" % h for h in range(NH)])
                                    p.pool(lambda e, sgc=sgc: e.tensor_tensor(out=T1[0:64, :], in0=sgc[0:64, :], in1=GN[0:64, :], op=ALU.mult),
                                          reads=["SGc%d" % (sb_ % NB), "GN"], writes=["T1"])
                                    yg = YG[sb_]
                                    for h in range(NH):
                                        p.dve(lambda e, h=h, ot=ot, yg=yg: e.scalar_tensor_tensor(
                                            out=yg[0:64, h * 256:(h + 1) * 256], in0=ot[0:64, h * 256:(h + 1) * 256], scalar=RSg[0:64, h:h + 1],
                                            in1=T1[0:64, h * 256:(h + 1) * 256], op0=ALU.mult, op1=ALU.mult),
                                            reads=okeys + ["RSg", "T1"], writes=["YG%d_%d" % (sb_ % NB, h)])
                                    nblk = VW // 128
                                    for j0 in range(0, nblk, 8):
                                        for j in range(j0, j0 + 8):
                                            p.pe(lambda e, j=j, j0=j0, yg=yg: e.transpose(PSY[:, (j - j0) * 64:(j - j0 + 1) * 64],
                                                                                         yg[0:64, j * 128:(j + 1) * 128], identb[0:64, 0:64]),
                                                 reads=["YG%d_%d" % (sb_ % NB, j // 2), "identb"], writes=["PS0"])
                                        p.act(lambda e, j0=j0, c=c: e.activation(out=YM[:, j0:j0 + 8, c * 64:(c + 1) * 64],
                                                                                  in_=PSY[:, 0:512].rearrange("p (j t) -> p j t", t=64), func=AF.Copy),
                                              reads=["PS0"], writes=["YM"])
                        return post

                    pending = None
                    for ci, c in enumerate(corder):
                        nxt = chunk(ci, c)
                        if pending is not None:
                            pending()
                        pending = nxt
                    if pending is not None:
                        pending()
                    if fwd and readout:
                        nb = VW // 128
                        p.dma("pool", ymixv[:, 0:nb, t0:t0 + N], YM[:, 0:nb, :N], reads=["YM"], writes=[])

                for ti in order:
                    slab(ti)

            for d in (1, 0):
                do_dir(d)

        full = STAGE == "full"
        for ti in range(len(tiles)):
            t0, N, w = tiles[ti]
            cur["N"] = N; cur["t0"] = t0
            load_x_tile(ti)
            ffn(0, w, 0, f1in_b[0], f1out_b[0], wkeys["f1in0"], wkeys["f1out0"], N)
            if ti == 0:
                conv_rest()
            if STAGE == "ffn1":
                if w == 0:
                    store_out_tile(ti)
                continue
            premix(0, w, N)
            inproj0()
            store_h(ti)
        if STAGE != "ffn1":
            barrier()
            mixer_rg()
            barrier()
            mixer_chunk(0)
            barrier()
            for ti in range(len(tiles)):
                t0, N, w = tiles[ti]
                cur["N"] = N; cur["t0"] = t0
                load_h(ti)
                outproj_residual(0, w, about_b, wkeys["about"], N, t0)
                if STAGE == "mix0":
                    if w == 0:
                        store_out_tile(ti)
                    continue
                ffn(0, w, 2, f2in_b[0], f2out_b[0], wkeys["f2in0"], wkeys["f2out0"], N)
                if STAGE == "l0":
                    if w == 0:
                        store_out_tile(ti)
                    continue
                ffn(1, w, 0, f1in_b[1], f1out_b[1], wkeys["f1in1"], wkeys["f1out1"], N)
                premix(1, w, N)
                inproj1()
                if w == 0:
                    store_h(ti)
        if full:
            barrier()
            mixer_chunk(1)
            barrier()
            for ti in range(1, len(tiles)):
                t0, N, w = tiles[ti]
                cur["N"] = N; cur["t0"] = t0
                load_h(ti)
                outproj_residual(1, w, mlout_b, wkeys["mlout"], N, t0)
                ffn(1, w, 2, f2in_b[1], f2out_b[1], wkeys["f2in1"], wkeys["f2out1"], N)
                store_out_tile(ti)
        p.emit(final_waits=("sp", "pool"))
    return nc


def make_in_maps(inputs):
    g = lambda k: np.asarray(inputs[k], dtype=np.float32)
    x, c, ctx, c_ctx = g("x"), g("c"), g("ctx"), g("c_ctx")
    shared = {
        "w_mod": np.ascontiguousarray(g("w_mod")),
        "ffn1_w_in": np.ascontiguousarray(g("ffn1_w_in")), "ffn1_w_out": np.ascontiguousarray(g("ffn1_w_out")),
        "ffn2_w_in": np.ascontiguousarray(g("ffn2_w_in")), "ffn2_w_out": np.ascontiguousarray(g("ffn2_w_out")),
        "ab_w_in": np.ascontiguousarray(g("ab_w_in")[0]), "ab_w_out": np.ascontiguousarray(g("ab_w_out")[0]),
        "ml_w_in": np.ascontiguousarray(g("ml_w_in")[0]), "ml_w_out": np.ascontiguousarray(g("ml_w_out")[0]),
    }
    shared["wa2"] = np.ascontiguousarray(np.concatenate([g("gla_w_alpha2")[0, 0], g("gla_w_alpha2")[0, 1]], axis=1))
    rw = np.stack([g("rg_w_a")[0], g("rg_w_i")[0]], axis=0)
    shared["rgw"] = np.ascontiguousarray(np.transpose(rw, (3, 0, 1, 2, 4)).reshape(128, 4096))
    shared["gn4"] = np.ascontiguousarray(np.broadcast_to(np.tile(g("gla_norm_g")[0], 4)[None, :], (128, 1024)))
    shared["mn8"] = np.ascontiguousarray(np.broadcast_to(np.tile(g("ml_norm_g")[0], 8)[None, :], (128, 2048)))
    mlbg = np.zeros((128, 1), np.float32)
    mlbg[:32, 0] = g("ml_b_gates")[0]
    in_maps = []
    for b in range(8):
        pk = np.concatenate([
            fm(c[b]), fm(c_ctx), fm(g("b_mod").reshape(2, 9 * D)), fm(g("norm_g").reshape(12, D)),
            fm(g("gla_b_alpha")[0]), fm(g("rg_conv_w")[0]), fm(g("rg_conv_b")[0]), fm(g("rg_b_a")[0]),
            fm(g("rg_b_i")[0]), fm(g("rg_lambda")[0]), mlbg], axis=1)
        assert pk.shape == (128, NPK), pk.shape
        m = dict(shared)
        m["x"] = np.ascontiguousarray(x[b])
        m["ctx"] = np.ascontiguousarray(ctx[b])
        m["pk"] = np.ascontiguousarray(pk.astype(np.float32))
        in_maps.append(m)
    return in_maps


_NC = None


def kernel(**inputs):
    global _NC
    if _NC is None:
        _NC = build()
    in_maps = make_in_maps(inputs)
    ncores = int(os.environ.get("MK_CORES", "8"))
    res = run_bass_kernel_spmd(_NC, in_maps[:ncores], core_ids=list(range(ncores)))
    outs = [np.asarray(r["out"], dtype=np.float32) for r in res.results]
    while len(outs) < 8:
        outs.append(np.zeros_like(outs[0]))
    return np.stack(outs, axis=0)
```

```python
import contextlib
import math
import os
import numpy as np
import concourse.bass as bass
import concourse.mybir as mybir
from concourse.bass_utils import run_bass_kernel_spmd

F32 = mybir.dt.float32
BF16 = mybir.dt.bfloat16
I32 = mybir.dt.int32
AF = mybir.ActivationFunctionType
ALU = mybir.AluOpType

ENGS = ("pe", "dve", "act", "pool", "sp")
NDMASEM = 48

D = 2048
KC = 16
NCTX = 256
NLAT = 4096
T = NCTX + NLAT
DFF = 5632
EPS = 1e-6
AB_COLS = 5152
ML_COLS = 6176


class Op:
    __slots__ = ("eng", "fn", "deps", "signal", "count", "is_dma", "dsem", "idx")

    def __init__(self, eng, fn, is_dma=False):
        self.eng = eng
        self.fn = fn
        self.deps = []
        self.signal = False
        self.count = None
        self.is_dma = is_dma
        self.dsem = None


class Prog:
    def __init__(self, nc):
        self.nc = nc
        self.ops = {e: [] for e in ENGS}
        self.last_w = {}
        self.readers = {}
        self.ndma = 0
        self.dma_ops = []
        self.nops = 0

    def _add(self, eng, fn, reads, writes, is_dma=False, extra=()):
        op = Op(eng, fn, is_dma)
        op.idx = self.nops
        self.nops += 1
        deps = list(extra)
        for r in reads:
            w = self.last_w.get(r)
            if w is not None:
                deps.append(w)
        for w_ in writes:
            w = self.last_w.get(w_)
            if w is not None:
                deps.append(w)
            deps.extend(self.readers.get(w_, ()))
        for w_ in writes:
            self.last_w[w_] = op
            self.readers[w_] = []
        for r in reads:
            self.readers.setdefault(r, []).append(op)
        if is_dma:
            k = self.ndma
            self.ndma += 1
            if k >= NDMASEM:
                deps.append(self.dma_ops[k - NDMASEM])
            self.dma_ops.append(op)
        seen = set()
        for d in deps:
            if d is op or id(d) in seen:
                continue
            seen.add(id(d))
            if (not d.is_dma) and d.eng == eng and eng == "pe":
                continue
            op.deps.append(d)
            d.signal = True
        self.ops[eng].append(op)
        return op

    def pe(self, fn, reads=(), writes=()):
        return self._add("pe", fn, reads, writes)

    def dve(self, fn, reads=(), writes=()):
        return self._add("dve", fn, reads, writes)

    def act(self, fn, reads=(), writes=()):
        return self._add("act", fn, reads, writes)

    def pool(self, fn, reads=(), writes=()):
        return self._add("pool", fn, reads, writes)

    def dma(self, eng, out, in_, reads=(), writes=(), **kw):
        return self._add(eng, lambda e: e.dma_start(out=out, in_=in_, **kw), reads, writes, is_dma=True)

    def emit(self, final_waits=("sp",)):
        nc = self.nc
        with contextlib.ExitStack() as st:
            esem = {e: st.enter_context(nc.semaphore("s_" + e)) for e in ENGS if e != "sp"}
            dsem = [st.enter_context(nc.semaphore("d%d" % i)) for i in range(NDMASEM)]
            block = st.enter_context(nc.Block())
            for e in ENGS:
                c = 0
                for op in self.ops[e]:
                    if op.is_dma:
                        continue
                    if op.signal:
                        c += 1
                        op.count = c
            dcount = [0] * NDMASEM
            for k, op in enumerate(self.dma_ops):
                s = k % NDMASEM
                dcount[s] += 16
                op.dsem = s
                op.count = dcount[s]

            def run(e_name, eng):
                waited = {}
                for op in self.ops[e_name]:
                    for d in op.deps:
                        if d.is_dma:
                            key = ("d", d.dsem)
                            sem = dsem[d.dsem]
                        else:
                            key = ("e", d.eng)
                            sem = esem[d.eng]
                        if waited.get(key, 0) >= d.count:
                            continue
                        waited[key] = d.count
                        eng.wait_ge(sem, d.count)
                    ins = op.fn(eng)
                    if op.is_dma:
                        ins.then_inc(dsem[op.dsem], 16)
                    elif op.signal:
                        ins.then_inc(esem[e_name], 1)
                if e_name in final_waits:
                    for k in range(NDMASEM):
                        if dcount[k] and waited.get(("d", k), 0) < dcount[k]:
                            eng.wait_ge(dsem[k], dcount[k])

            @block.tensor
            def _(eng):
                run("pe", eng)

            @block.vector
            def _(eng):
                run("dve", eng)

            @block.scalar
            def _(eng):
                run("act", eng)

            @block.gpsimd
            def _(eng):
                run("pool", eng)

            @block.sync
            def _(eng):
                run("sp", eng)


def fm(v):
    v = np.asarray(v)
    sh = v.shape
    v2 = v.reshape(sh[:-1] + (sh[-1] // 128, 128))
    return np.ascontiguousarray(np.moveaxis(v2, -1, 0)).reshape(128, -1)


PK = {}
_o = 0
for _n, _w in (("c", 16), ("cctx", 16), ("bmod", 2 * 144), ("ng", 2 * 6 * 16), ("balpha", 8),
               ("convw", 32), ("convb", 8), ("rgba", 16), ("rgbi", 16), ("rglam", 16), ("mlbg", 1)):
    PK[_n] = (_o, _w)
    _o += _w
NPK = _o

SBW = 46400

STAGE = os.environ.get("MK_STAGE", "full")


def build():
    nc = bass.Bass("TRN2", target_bir_lowering=False)
    dt_in = lambda n, s, d=F32: nc.dram_tensor(n, s, d, kind="ExternalInput").ap()
    dt_sc = lambda n, s, d: nc.dram_tensor(n, s, d, kind="Internal").ap()
    x_d = dt_in("x", [NLAT, D])
    ctx_d = dt_in("ctx", [NCTX, D])
    pk_d = dt_in("pk", [128, NPK])
    wa2_d = dt_in("wa2", [16, 1024])
    rgw_d = dt_in("rgw", [128, 4096])
    gn4_d = dt_in("gn4", [128, 1024])
    mn8_d = dt_in("mn8", [128, 2048])
    wmod_d = dt_in("w_mod", [2, D, 9 * D])
    f1in_d = dt_in("ffn1_w_in", [2, D, 2 * DFF])
    f1out_d = dt_in("ffn1_w_out", [2, DFF, D])
    f2in_d = dt_in("ffn2_w_in", [2, D, 2 * DFF])
    f2out_d = dt_in("ffn2_w_out", [2, DFF, D])
    abin_d = dt_in("ab_w_in", [D, AB_COLS])
    about_d = dt_in("ab_w_out", [D, D])
    mlin_d = dt_in("ml_w_in", [D, ML_COLS])
    mlout_d = dt_in("ml_w_out", [D, D])
    out_d = nc.dram_tensor("out", [NLAT, D], F32, kind="ExternalOutput").ap()

    f1in_b = [dt_sc("f1in_b%d" % l, [D, 2 * DFF], BF16) for l in range(2)]
    f1out_b = [dt_sc("f1out_b%d" % l, [DFF, D], BF16) for l in range(2)]
    f2in_b = [dt_sc("f2in_b%d" % l, [D, 2 * DFF], BF16) for l in range(2)]
    f2out_b = [dt_sc("f2out_b%d" % l, [DFF, D], BF16) for l in range(2)]
    abin_b = dt_sc("abin_b", [D, AB_COLS], BF16)
    about_b = dt_sc("about_b", [D, D], BF16)
    mlin_b = dt_sc("mlin_b", [D, ML_COLS], BF16)
    mlout_b = dt_sc("mlout_b", [D, D], BF16)
    hfm_d = dt_sc("hfm", [D, T], F32)
    ymix_d = dt_sc("ymix", [D, T], BF16)

    with contextlib.ExitStack() as st:
        SB = st.enter_context(nc.sbuf_tensor("SB", [128, SBW], F32))
        PS = [st.enter_context(nc.psum_tensor("ps%d" % i, [128, 512], F32)) for i in range(7)]
        PSB = st.enter_context(nc.psum_tensor("psb", [128, 1024], BF16))
        p = Prog(nc)

        top = [0]

        def alloc(nwords):
            o = top[0]
            top[0] += nwords
            assert top[0] <= SBW, ("SBUF arena overflow", top[0])
            return o

        def V(off, nwords, dt=F32, pat=None, **kw):
            ap = SB[:, off:off + nwords]
            if dt is not F32:
                ap = ap.bitcast(dt)
            if pat:
                ap = ap.rearrange(pat, **kw)
            return ap

        o_identf = alloc(128); identf = V(o_identf, 128)
        o_identb = alloc(64); identb = V(o_identb, 64, BF16)
        o_onesb = alloc(64); onesb = V(o_onesb, 64, BF16)
        o_pk = alloc(NPK); pk = V(o_pk, NPK)
        o_modraw = alloc(576); modraw = V(o_modraw, 576, F32, "p (l n w) -> p l n w", l=2, w=2)
        o_TA = alloc(192); TA = V(o_TA, 192, F32, "p (l w s c) -> p l w s c", l=2, w=2, s=3)
        o_TS = alloc(192); TS = V(o_TS, 192, F32, "p (l w s c) -> p l w s c", l=2, w=2, s=3)
        o_TG = alloc(192); TG = V(o_TG, 192, F32, "p (l w s c) -> p l w s c", l=2, w=2, s=3)
        o_scb = alloc(16); scb = V(o_scb, 16, BF16, "p (k w) -> p k w", w=2)
        o_postab = alloc(512); postab = V(o_postab, 512, F32, "p (c j) -> p c j", j=64)
        o_scr = alloc(8); scr = V(o_scr, 8)
        o_tmpf = alloc(128); tmpf = V(o_tmpf, 128)

        def pkc(name, j=0, n=None):
            o, w = PK[name]
            n = w - j if n is None else n
            return pk[:, o + j:o + j + n]

        base_top = top[0]

        p.dma("sp", pk, pk_d[:, :], writes=["pk"])
        p.pool(lambda e: e.memset(identf, 0.0), writes=["identf"])
        p.pool(lambda e: e.affine_select(out=identf, in_=identf, pattern=[[-1, 128]], compare_op=ALU.not_equal,
                                         fill=1.0, base=0, channel_multiplier=1), reads=["identf"], writes=["identf"])
        p.dve(lambda e: e.tensor_copy(out=identb, in_=identf), reads=["identf"], writes=["identb"])
        p.dve(lambda e: e.memset(onesb, 1.0), writes=["onesb"])
        p.dve(lambda e: e.memset(scr, 0.0), writes=["scr"])

        conv_prev = []

        def convert(src, dst, cc, key):
            R, C = src.shape
            per = C // cc
            rows = max(1, 3500 // per)
            r0 = 0
            keys = []
            i = 0
            while r0 < R:
                r1 = min(R, r0 + rows)
                s_ = src[r0:r1, :].rearrange("r (a c) -> r a c", c=cc)
                d_ = dst[r0:r1, :].rearrange("r (a c) -> r a c", c=cc)
                k = "%s_%d" % (key, i)
                p.dma("pool", d_, s_, reads=list(conv_prev[-2:-1]), writes=[k])
                conv_prev.append(k)
                keys.append(k)
                r0 = r1
                i += 1
            return keys

        wkeys = {}
        wkeys["f1in0"] = convert(f1in_d[0], f1in_b[0], 1408, "f1in0")
        wkeys["f1out0"] = convert(f1out_d[0], f1out_b[0], 2048, "f1out0")

        def conv_A():
            wkeys["abin"] = convert(abin_d, abin_b, 1288, "abin")
            wkeys["about"] = convert(about_d, about_b, 2048, "about")

        def conv_B():
            wkeys["f2in0"] = convert(f2in_d[0], f2in_b[0], 1408, "f2in0")
            wkeys["f2out0"] = convert(f2out_d[0], f2out_b[0], 2048, "f2out0")
            wkeys["f1in1"] = convert(f1in_d[1], f1in_b[1], 1408, "f1in1")
            wkeys["f1out1"] = convert(f1out_d[1], f1out_b[1], 2048, "f1out1")
            wkeys["mlin"] = convert(mlin_d, mlin_b, 1544, "mlin")

        def conv_C():
            wkeys["mlout"] = convert(mlout_d, mlout_b, 2048, "mlout")
            wkeys["f2in1"] = convert(f2in_d[1], f2in_b[1], 1408, "f2in1")
            wkeys["f2out1"] = convert(f2out_d[1], f2out_b[1], 2048, "f2out1")

        conv_A()

        p.act(lambda e: e.activation(out=scb[:, :, 0], in_=pkc("c"), func=AF.Silu), reads=["pk"], writes=["scb0"])
        p.act(lambda e: e.activation(out=scb[:, :, 1], in_=pkc("cctx"), func=AF.Silu), reads=["pk"], writes=["scb1"])
        o_wf = [alloc(8192) for _ in range(4)]
        WF = [V(o, 8192, F32, "p (k n) -> p k n", n=1024) for o in o_wf]
        o_wm = alloc(8192)
        WM = V(o_wm, 8192, BF16, "p (k n) -> p k n", n=1024)
        PSM = PS[6]
        wfc = [0]
        for l in range(2):
            wv = wmod_d[l].rearrange("(k p) n -> p k n", p=128)
            for g in range(18):
                wmk = []
                for kh in range(2):
                    b = wfc[0] % 4
                    wfc[0] += 1
                    p.dma("sp", WF[b], wv[:, kh * 8:(kh + 1) * 8, g * 1024:(g + 1) * 1024], writes=["WF%d" % b])
                    for k4 in range(2):
                        o_ = WM[:, kh * 8 + k4 * 4:kh * 8 + (k4 + 1) * 4, :]
                        i_ = WF[b][:, k4 * 4:(k4 + 1) * 4, :]
                        key = "WM_%d_%d" % (kh, k4)
                        wmk.append(key)
                        if k4 % 2 == 0:
                            p.dve(lambda e, o_=o_, i_=i_: e.tensor_copy(out=o_, in_=i_), reads=["WF%d" % b], writes=[key])
                        else:
                            p.act(lambda e, o_=o_, i_=i_: e.activation(out=o_, in_=i_, func=AF.Copy), reads=["WF%d" % b], writes=[key])
                for j in range(8):
                    blk = g * 8 + j
                    for k in range(KC):
                        first = k == 0
                        last = k == KC - 1
                        p.pe(lambda e, j=j, k=k, blk=blk, first=first, last=last:
                             e.matmul(PSM[:, blk * 2:blk * 2 + 2], WM[:, k, j * 128:(j + 1) * 128], scb[:, k, :],
                                      start=first, stop=last),
                             reads=(wmk + ["scb0", "scb1"]) if (first or last) else (),
                             writes=["PSM"] if ((first and blk == 0) or (last and blk == 143)) else ())
            bm = pkc("bmod", l * 144, 144)
            bm_b = bass.AP(bm.tensor, bm.offset, [list(bm.ap[0]), [1, 144], [0, 2]])
            p.dve(lambda e, l=l, bm_b=bm_b: e.tensor_tensor(out=modraw[:, l, :, :],
                                                          in0=PSM[:, 0:288].rearrange("p (n w) -> p n w", w=2),
                                                          in1=bm_b, op=ALU.add),
                  reads=["PSM", "pk"], writes=["modraw"])
        for l in range(2):
            for w in range(2):
                for s in range(3):
                    ngpre = pkc("ng", (l * 6 + 2 * s) * 16, 16)
                    ngpost = pkc("ng", (l * 6 + 2 * s + 1) * 16, 16)
                    wgt = 1.0 if s == 1 else 0.5
                    p.dve(lambda e, l=l, w=w, s=s, ngpre=ngpre: e.scalar_tensor_tensor(
                        out=TA[:, l, w, s, :], in0=modraw[:, l, (3 * s + 1) * 16:(3 * s + 2) * 16, w], scalar=1.0,
                        in1=ngpre, op0=ALU.add, op1=ALU.mult), reads=["modraw", "pk"], writes=["TA"])
                    p.dve(lambda e, l=l, w=w, s=s: e.tensor_copy(
                        out=TS[:, l, w, s, :], in_=modraw[:, l, (3 * s) * 16:(3 * s + 1) * 16, w]),
                        reads=["modraw"], writes=["TS"])
                    p.dve(lambda e, l=l, w=w, s=s, ngpost=ngpost, wgt=wgt: e.scalar_tensor_tensor(
                        out=TG[:, l, w, s, :], in0=modraw[:, l, (3 * s + 2) * 16:(3 * s + 3) * 16, w], scalar=wgt,
                        in1=ngpost, op0=ALU.mult, op1=ALU.mult), reads=["modraw", "pk"], writes=["TG"])

        o_pt = alloc(2048)
        fidx_i = V(o_pt, 4, I32)
        posi = V(o_pt + 8, 64, I32)
        fidx = V(o_pt + 80, 4)
        posf = V(o_pt + 96, 64)
        omega = V(o_pt + 160, 4)
        ang = V(o_pt + 256, 256, F32, "p (c j) -> p c j", j=64)
        tt = V(o_pt + 512, 256)
        ki = V(o_pt + 768, 256, I32)
        kf = V(o_pt + 1024, 256)
        w1 = V(o_pt + 1280, 256)
        gg = V(o_pt + 1536, 256)
        p.pool(lambda e: e.iota(fidx_i, pattern=[[128, 4]], base=0, channel_multiplier=1), writes=["fidx_i"])
        p.pool(lambda e: e.iota(posi, pattern=[[1, 64]], base=0, channel_multiplier=0), writes=["posi"])
        p.dve(lambda e: e.tensor_copy(out=fidx, in_=fidx_i), reads=["fidx_i"], writes=["fidx"])
        p.dve(lambda e: e.tensor_copy(out=posf, in_=posi), reads=["posi"], writes=["posf"])
        p.act(lambda e: e.activation(out=omega, in_=fidx, func=AF.Exp, scale=-math.log(10000.0) / 512.0),
              reads=["fidx"], writes=["omega"])
        for j in range(4):
            p.dve(lambda e, j=j: e.tensor_scalar(ang[:, j, :], posf, omega[:, j:j + 1], None, ALU.mult),
                  reads=["posf", "omega"], writes=["ang"])
        angf = V(o_pt + 256, 256)
        for half in range(2):
            p.dve(lambda e, half=half: e.tensor_scalar(tt, angf, 1.0 / (2 * math.pi), 0.25 * half, ALU.mult, ALU.add),
                  reads=["ang"], writes=["tt"])
            p.dve(lambda e: e.tensor_copy(out=ki, in_=tt), reads=["tt"], writes=["ki"])
            p.dve(lambda e: e.tensor_copy(out=kf, in_=ki), reads=["ki"], writes=["kf"])
            p.dve(lambda e: e.tensor_tensor(out=w1, in0=tt, in1=kf, op=ALU.subtract), reads=["tt", "kf"], writes=["w1"])
            p.dve(lambda e: e.tensor_single_scalar(gg, w1, 0.5, ALU.is_gt), reads=["w1"], writes=["gg"])
            p.dve(lambda e: e.tensor_tensor(out=w1, in0=w1, in1=gg, op=ALU.subtract), reads=["w1", "gg"], writes=["w1"])
            p.dve(lambda e: e.tensor_single_scalar(gg, w1, -0.5, ALU.is_lt), reads=["w1"], writes=["gg"])
            p.dve(lambda e: e.tensor_tensor(out=w1, in0=w1, in1=gg, op=ALU.add), reads=["w1", "gg"], writes=["w1"])
            p.act(lambda e, half=half: e.activation(out=postab[:, half * 4:(half + 1) * 4, :],
                                                    in_=w1.rearrange("p (c j) -> p c j", j=64), func=AF.Sin,
                                                    scale=2 * math.pi * (1 - 1e-6)),
                  reads=["w1"], writes=["postab"])

        def barrier():
            lasts = []
            for e in ENGS:
                for op in reversed(p.ops[e]):
                    if not op.is_dma:
                        lasts.append(op)
                        break
            ex = lasts + p.dma_ops[-NDMASEM:]
            p._add("dve", lambda e: e.memset(scr[:, 0:1], 0.0), (), ("_b0",), extra=ex)
            p._add("act", lambda e: e.activation(out=scr[:, 1:2], in_=scr[:, 2:3], func=AF.Copy), (), ("_b1",), extra=ex)
            p._add("pool", lambda e: e.memset(scr[:, 3:4], 0.0), (), ("_b2",), extra=ex)
            p._add("pe", lambda e: e.matmul(PS[6][0:2, 0:2], scb[:, 0, :], scb[:, 0, :], start=True, stop=True),
                   (), ("_b3",), extra=ex)
            p._add("sp", lambda e: e.dma_start(out=scr[0:1, 5:7], in_=pk_d[0:1, 0:2]), (), ("_b4",), is_dma=True, extra=ex)
            p.last_w = {k: v for k, v in p.last_w.items() if isinstance(k, str) and k.startswith(("hfm", "ymix", "f1", "f2", "ab", "ml", "sc_"))}
            p.readers = {k: v for k, v in p.readers.items() if k in p.last_w}
            for k in ("_b0", "_b1", "_b2", "_b3"):
                pass
            p._bar = [p.ops["dve"][-1], p.ops["act"][-1], p.ops["pool"][-1], p.ops["pe"][-1]]

        barrier()
        top[0] = base_top

        o_H = alloc(8192); H = V(o_H, 8192, F32, "p (c n) -> p c n", n=512)
        o_Y = alloc(8192); Y = V(o_Y, 8192, F32, "p (c n) -> p c n", n=512)
        XB = [V(o_Y + i * 2048, 2048) for i in range(2)]
        o_U = alloc(4096); U = V(o_U, 4096, BF16, "p (c n) -> p c n", n=512)
        o_HID = alloc(5632); HID = V(o_HID, 5632, BF16, "p (f n) -> p f n", n=512)
        o_WI = [alloc(4096), alloc(4096)]
        WI = [V(o, 4096, BF16, "p (k n) -> p k n", n=512) for o in o_WI]
        o_WO = [alloc(2816), alloc(2816)]
        WO = [V(o, 2816, BF16, "p (f n) -> p f n", n=256) for o in o_WO]
        o_RS = alloc(512); RS = V(o_RS, 512)
        o_RS2 = alloc(512); RS2 = V(o_RS2, 512)
        o_TMP = [alloc(512), alloc(512)]; TMP = [V(o, 512) for o in o_TMP]
        o_SA = [alloc(512), alloc(512)]; SA = [V(o, 512) for o in o_SA]
        o_SQ = [alloc(256), alloc(256)]; SQ = [V(o, 256, BF16) for o in o_SQ]
        wictr = [0]
        woctr = [0]
        psctr = [0]

        tiles = [(0, NCTX, 1)] + [(NCTX + 512 * i, 512, 0) for i in range(8)]

        def bar_deps():
            return p._bar

        first_after_bar = {e: True for e in ENGS}

        def stats_chunk(src, rkeys, N, c):
            PSS = PS[6]
            b = c % 2
            p.act(lambda e, c=c, b=b: e.activation(out=SQ[b][:, :N], in_=src(c), func=AF.Square),
                  reads=rkeys(c), writes=["SQ%d" % b])
            p.pe(lambda e, c=c, b=b: e.matmul(PSS[:, :N], onesb, SQ[b][:, :N], start=(c == 0), stop=(c == KC - 1)),
                 reads=["SQ%d" % b, "onesb"], writes=["PSS"] if c in (0, KC - 1) else ())

        def stats_finish(N, rs_out, rs_key):
            PSS = PS[6]
            p.act(lambda e: e.activation(out=TMP[0][:, :N], in_=PSS[:, :N], func=AF.Sqrt, bias=scr[:, 4:5], scale=1.0 / D),
                  reads=["PSS", "eps"], writes=["TMP0"])
            p.dve(lambda e: e.reciprocal(rs_out[:, :N], TMP[0][:, :N]), reads=["TMP0"], writes=[rs_key])

        def stats(src, rkeys, N, rs_out, rs_key):
            for c in range(KC):
                stats_chunk(src, rkeys, N, c)
            stats_finish(N, rs_out, rs_key)

        p.dve(lambda e: e.memset(scr[:, 4:5], EPS), writes=["eps"])

        def modulate_to_U(l, w, s, N):
            for c in range(KC):
                b = c % 2
                p.dve(lambda e, c=c, b=b: e.scalar_tensor_tensor(out=TMP[b][:, :N], in0=H[:, c, :N],
                                                                 scalar=TA[:, l, w, s, c:c + 1], in1=RS[:, :N],
                                                                 op0=ALU.mult, op1=ALU.mult),
                      reads=["H%d" % c, "RS", "TA"], writes=["TMP%d" % b])
                p.act(lambda e, c=c, b=b: e.activation(out=U[:, c, :N], in_=TMP[b][:, :N], func=AF.Identity,
                                                       bias=TS[:, l, w, s, c:c + 1], scale=1.0),
                      reads=["TMP%d" % b, "TS"], writes=["U%d" % c])

        HK = ["H%d" % c for c in range(KC)]
        UK = ["U%d" % c for c in range(KC)]
        YK = ["Y%d" % c for c in range(KC)]

        def residual(l, w, s, N):
            for c in range(KC):
                b = c % 2
                p.dve(lambda e, c=c, b=b: e.scalar_tensor_tensor(out=TMP[b][:, :N], in0=Y[:, c, :N],
                                                                 scalar=TG[:, l, w, s, c:c + 1], in1=RS2[:, :N],
                                                                 op0=ALU.mult, op1=ALU.mult),
                      reads=["Y%d" % c, "RS2", "TG"], writes=["TMP%d" % b])
                p.dve(lambda e, c=c, b=b: e.tensor_tensor(out=H[:, c, :N], in0=H[:, c, :N], in1=TMP[b][:, :N], op=ALU.add),
                      reads=["TMP%d" % b, "H%d" % c], writes=["H%d" % c])

        def ffn(l, w, s, win_b, wout_b, win_keys, wout_keys, N):
            stats(lambda c: H[:, c, :N], lambda c: ["H%d" % c], N, RS, "RS")
            modulate_to_U(l, w, s, N)
            winv = win_b.rearrange("(k p) n -> p k n", p=128)
            woutv = wout_b.rearrange("(f p) n -> p f n", p=128)
            for half in range(2):
                for g in range(11):
                    ff0 = half * 22 + g * 2
                    wb = wictr[0] % 2
                    wictr[0] += 1
                    p.dma("sp", WI[wb][:, :, 0:256], winv[:, :, ff0 * 128:ff0 * 128 + 256], reads=win_keys,
                          writes=["WI%d" % wb])
                    p.dma("sp", WI[wb][:, :, 256:512], winv[:, :, DFF + ff0 * 128:DFF + ff0 * 128 + 256],
                          reads=win_keys, writes=["WI%db" % wb])
                    for j in range(2):
                        pa = PS[(psctr[0] % 2) * 2]
                        pb = PS[(psctr[0] % 2) * 2 + 1]
                        ka = "PS%d" % ((psctr[0] % 2) * 2)
                        kb = "PS%d" % ((psctr[0] % 2) * 2 + 1)
                        psctr[0] += 1
                        for (pt, kk, off) in ((pa, ka, 0), (pb, kb, 256)):
                            for k in range(KC):
                                fl = k in (0, KC - 1)
                                p.pe(lambda e, pt=pt, wb=wb, k=k, off=off, j=j: e.matmul(
                                    pt[:, :N], WI[wb][:, k, off + j * 128:off + (j + 1) * 128], U[:, k, :N],
                                    start=(k == 0), stop=(k == KC - 1)),
                                    reads=["U%d" % k] + ((["WI%d" % wb, "WI%db" % wb] + UK) if fl else []),
                                    writes=[kk] if fl else ())
                        sb = (g * 2 + j) % 2
                        p.act(lambda e, pa=pa, sb=sb: e.activation(out=SA[sb][:, :N], in_=pa[:, :N], func=AF.Silu),
                              reads=[ka], writes=["SA%d" % sb])
                        f = g * 2 + j
                        p.dve(lambda e, pb=pb, sb=sb, f=f: e.tensor_tensor(out=HID[:, f, :N], in0=SA[sb][:, :N],
                                                                           in1=pb[:, :N], op=ALU.mult),
                              reads=[kb, "SA%d" % sb], writes=["HID%d" % f])
                HIDK = ["HID%d" % f for f in range(22)]
                for cg in range(8):
                    wb = woctr[0] % 2
                    woctr[0] += 1
                    p.dma("sp", WO[wb], woutv[:, half * 22:(half + 1) * 22, cg * 256:(cg + 1) * 256], reads=wout_keys,
                          writes=["WO%d" % wb])
                    for j in range(2):
                        c = cg * 2 + j
                        py = PS[4 + c % 2]
                        ky = "PS%d" % (4 + c % 2)
                        for f in range(22):
                            fl = f in (0, 21)
                            p.pe(lambda e, py=py, wb=wb, f=f, j=j: e.matmul(
                                py[:, :N], WO[wb][:, f, j * 128:(j + 1) * 128], HID[:, f, :N],
                                start=(f == 0), stop=(f == 21)),
                                reads=(["WO%d" % wb] + HIDK) if fl else (), writes=[ky] if fl else ())
                        if half == 0:
                            p.act(lambda e, py=py, c=c: e.activation(out=Y[:, c, :N], in_=py[:, :N], func=AF.Copy),
                                  reads=[ky], writes=["Y%d" % c])
                        else:
                            p.dve(lambda e, py=py, c=c: e.tensor_tensor(out=Y[:, c, :N], in0=Y[:, c, :N], in1=py[:, :N],
                                                                        op=ALU.add),
                                  reads=[ky, "Y%d" % c], writes=["Y%d" % c])
                            if c >= 2:
                                stats_chunk(lambda c_: Y[:, c_, :N], lambda c_: ["Y%d" % c_], N, c - 2)
            for c in (KC - 2, KC - 1):
                stats_chunk(lambda c_: Y[:, c_, :N], lambda c_: ["Y%d" % c_], N, c)
            stats_finish(N, RS2, "RS2")
            residual(l, w, s, N)

        def load_x_tile(ti):
            t0, N, w = tiles[ti]
            for tb in range(N // 128):
                xb = XB[tb % 2]
                kx = "Y%d" % (tb % 2)
                src = ctx_d[tb * 128:(tb + 1) * 128, :] if w == 1 else x_d[t0 - NCTX + tb * 128:t0 - NCTX + (tb + 1) * 128, :]
                xkeys = ["Y%d" % c for c in range(4 * (tb % 2), 4 * (tb % 2) + 4)]
                p.dma("sp", xb, src, writes=xkeys)
                for g in range(4):
                    pt = PS[g % 4]
                    kp = "PS%d" % (g % 4)
                    for q in range(4):
                        c = g * 4 + q
                        p.pe(lambda e, pt=pt, xb=xb, c=c, q=q: e.transpose(pt[:, q * 128:(q + 1) * 128],
                                                                          xb[:, c * 128:(c + 1) * 128], identf),
                             reads=xkeys + ["identf"], writes=[kp])
                    hout = H[:, g * 4:(g + 1) * 4, tb * 128:(tb + 1) * 128]
                    hk = ["H%d" % c for c in range(g * 4, g * 4 + 4)]
                    if w == 1:
                        p.dve(lambda e, pt=pt, hout=hout: e.tensor_copy(out=hout, in_=pt[:, :].rearrange("p (q t) -> p q t", t=128)),
                              reads=[kp], writes=hk)
                    else:
                        r0 = (t0 - NCTX + tb * 128) // 64
                        pa = postab[:, 0, :]
                        if g < 2:
                            in1 = bass.AP(pa.tensor, pa.offset + (g * 4) * 64 + r0, [list(pa.ap[0]), [64, 4], [1, 2], [0, 64]])
                        else:
                            in1 = bass.AP(pa.tensor, pa.offset + ((g - 2) * 4) * 64, [list(pa.ap[0]), [64, 4], [0, 2], [1, 64]])
                        ho = hout.rearrange("p q (r t) -> p q r t", t=64)
                        pi = pt[:, :].rearrange("p (q r t) -> p q r t", q=4, t=64)
                        p.dve(lambda e, pi=pi, ho=ho, in1=in1: e.tensor_tensor(out=ho, in0=pi, in1=in1, op=ALU.add),
                              reads=[kp, "postab"], writes=hk)

        def store_out_tile(ti):
            t0, N, w = tiles[ti]
            for tb in range(N // 128):
                ob = XB[tb % 2]
                okeys = ["Y%d" % c for c in range(4 * (tb % 2), 4 * (tb % 2) + 4)]
                for g in range(4):
                    pt = PS[g % 4]
                    kp = "PS%d" % (g % 4)
                    for q in range(4):
                        c = g * 4 + q
                        p.pe(lambda e, pt=pt, c=c, q=q, tb=tb: e.transpose(pt[:, q * 128:(q + 1) * 128],
                                                                          H[:, c, tb * 128:(tb + 1) * 128], identf),
                             reads=["H%d" % c, "identf"], writes=[kp])
                    if g % 2 == 0:
                        p.dve(lambda e, pt=pt, ob=ob, g=g: e.tensor_copy(out=ob[:, g * 512:(g + 1) * 512], in_=pt[:, :]),
                              reads=[kp], writes=[okeys[g]])
                    else:
                        p.act(lambda e, pt=pt, ob=ob, g=g: e.activation(out=ob[:, g * 512:(g + 1) * 512], in_=pt[:, :], func=AF.Copy),
                              reads=[kp], writes=[okeys[g]])
                r0 = t0 - NCTX + tb * 128
                p.dma("pool", out_d[r0:r0 + 128, :], ob, reads=okeys, writes=["out%d" % r0])

        hfmv = hfm_d.rearrange("(c p) t -> p c t", p=128)

        def store_h(ti):
            t0, N, w = tiles[ti]
            p.dma("pool", hfmv[:, :, t0:t0 + N], H[:, :, :N], reads=HK, writes=["hfm%d" % ti])

        def load_h(ti):
            t0, N, w = tiles[ti]
            p.dma("sp", H[:, :, :N], hfmv[:, :, t0:t0 + N], reads=["hfm%d" % ti], writes=HK)


        qT0 = dt_sc("qT0", [512, T], BF16); kT0 = dt_sc("kT0", [512, T], BF16)
        xrT = dt_sc("xrT", [1024, T], F32); grT = dt_sc("grT", [1024, T], BF16)
        lrT = dt_sc("lrT", [32, T], F32)
        v0 = dt_sc("v0", [T, 1024], BF16); sg0 = dt_sc("sg0", [T, 1024], BF16)
        ob0 = dt_sc("ob0", [T, 1024], F32)
        qT1 = dt_sc("qT1", [1024, T], BF16); kT1 = dt_sc("kT1", [1024, T], BF16)
        ktm1 = dt_sc("ktm1", [T, 1024], BF16); v1 = dt_sc("v1", [T, 2048], BF16)
        so1 = dt_sc("so1", [T, 2048], BF16); gT1 = dt_sc("gT1", [32, T], F32)
        hb1 = dt_sc("hb1", [T, 2048], F32)
        ymixv = ymix_d.rearrange("(c p) t -> p c t", p=128)

        STG = [(SA[0], "SA0"), (SA[1], "SA1"), (TMP[0], "TMP0"), (TMP[1], "TMP1")]
        stgc = [0]

        def stg():
            b_ = STG[stgc[0] % 4]
            stgc[0] += 1
            return b_

        cur = {}

        def proj_fm(wb_dram, wk, col0, ncols, evac):
            N = cur["N"]
            wv = wb_dram.rearrange("(k p) n -> p k n", p=128)
            c = col0
            while c < col0 + ncols:
                gw = min(512, col0 + ncols - c)
                wb = wictr[0] % 2
                wictr[0] += 1
                p.dma("sp", WI[wb][:, :, 0:gw], wv[:, :, c:c + gw], reads=wk, writes=["WI%d" % wb, "WI%db" % wb])
                for j0 in range(0, gw, 128):
                    m = min(128, gw - j0)
                    bank = psctr[0] % 4
                    psctr[0] += 1
                    pt = PS[bank]
                    kp = "PS%d" % bank
                    for k in range(KC):
                        fl = k in (0, KC - 1)
                        p.pe(lambda e, pt=pt, wb=wb, k=k, j0=j0, m=m: e.matmul(
                            pt[0:m, :N], WI[wb][:, k, j0:j0 + m], U[:, k, :N], start=(k == 0), stop=(k == KC - 1)),
                            reads=(["WI%d" % wb, "WI%db" % wb] + UK) if fl else (), writes=[kp] if fl else ())
                    evac(pt, kp, m, c + j0)
                c += gw

        def proj_tm(wb_dram, wk, col0, ncols, evac):
            N = cur["N"]
            wv = wb_dram.rearrange("(k p) n -> p k n", p=128)
            for c in range(col0, col0 + ncols, 512):
                wb = wictr[0] % 2
                wictr[0] += 1
                p.dma("sp", WI[wb][:, :, 0:512], wv[:, :, c:c + 512], reads=wk, writes=["WI%d" % wb, "WI%db" % wb])
                for tb in range(N // 128):
                    bank = psctr[0] % 4
                    psctr[0] += 1
                    pt = PS[bank]
                    kp = "PS%d" % bank
                    for k in range(KC):
                        fl = k in (0, KC - 1)
                        p.pe(lambda e, pt=pt, wb=wb, k=k, tb=tb: e.matmul(
                            pt[:, 0:512], U[:, k, tb * 128:(tb + 1) * 128], WI[wb][:, k, 0:512],
                            start=(k == 0), stop=(k == KC - 1)),
                            reads=(["WI%d" % wb, "WI%db" % wb] + UK) if fl else (), writes=[kp] if fl else ())
                    evac(pt, kp, tb, c)

        def ev_fm(dst, colbase, dt, func=None, scale=1.0, bias=None):
            def f(pt, kp, m, col):
                N = cur["N"]; t0 = cur["t0"]
                buf, key = stg()
                sv = buf if dt is F32 else buf.bitcast(BF16)
                if func is not None:
                    if bias is not None:
                        p.act(lambda e: e.activation(out=sv[0:m, :N], in_=pt[0:m, :N], func=func, bias=bias[0:m, :], scale=scale),
                              reads=[kp, "pk"], writes=[key])
                    else:
                        p.act(lambda e: e.activation(out=sv[0:m, :N], in_=pt[0:m, :N], func=func, scale=scale),
                              reads=[kp], writes=[key])
                else:
                    p.dve(lambda e: e.tensor_copy(out=sv[0:m, :N], in_=pt[0:m, :N]), reads=[kp], writes=[key])
                r = col - colbase
                p.dma("pool", dst[r:r + m, t0:t0 + N], sv[0:m, :N], reads=[key], writes=[])
            return f

        def ev_tm(dst, colbase, func=None):
            def f(pt, kp, tb, col):
                t0 = cur["t0"]
                buf, key = stg()
                sv = buf.bitcast(BF16)
                if func is not None:
                    p.act(lambda e: e.activation(out=sv[:, 0:512], in_=pt[:, 0:512], func=func), reads=[kp], writes=[key])
                else:
                    p.dve(lambda e: e.tensor_copy(out=sv[:, 0:512], in_=pt[:, 0:512]), reads=[kp], writes=[key])
                cc_ = col - colbase
                p.dma("pool", dst[t0 + tb * 128:t0 + (tb + 1) * 128, cc_:cc_ + 512], sv[:, 0:512], reads=[key], writes=[])
            return f

        QS = 128.0 ** -0.5

        def inproj0():
            wk = wkeys["abin"]
            proj_fm(abin_b, wk, 0, 512, ev_fm(qT0, 0, BF16, AF.Copy, QS))
            proj_fm(abin_b, wk, 512, 512, ev_fm(kT0, 512, BF16))
            proj_tm(abin_b, wk, 1024, 1024, ev_tm(v0, 1024))
            proj_tm(abin_b, wk, 2048, 1024, ev_tm(sg0, 2048, AF.Silu))
            proj_fm(abin_b, wk, 3072, 32, ev_fm(lrT, 3072, F32))
            proj_fm(abin_b, wk, 3104, 1024, ev_fm(xrT, 3104, F32))
            proj_fm(abin_b, wk, 4128, 1024, ev_fm(grT, 4128, BF16, AF.Gelu_apprx_tanh))

        def inproj1():
            wk = wkeys["mlin"]
            proj_fm(mlin_b, wk, 0, 1024, ev_fm(qT1, 0, BF16, AF.Copy, QS))
            proj_fm(mlin_b, wk, 1024, 1024, ev_fm(kT1, 1024, BF16))
            proj_tm(mlin_b, wk, 1024, 1024, ev_tm(ktm1, 1024))
            proj_tm(mlin_b, wk, 2048, 2048, ev_tm(v1, 2048))
            proj_tm(mlin_b, wk, 4096, 2048, ev_tm(so1, 4096, AF.Sigmoid))
            proj_fm(mlin_b, wk, 6144, 32, ev_fm(gT1, 6144, F32, AF.Identity, 1.0, pkc("mlbg")))

        def premix(l, w, N):
            stats(lambda c: H[:, c, :N], lambda c: ["H%d" % c], N, RS, "RS")
            modulate_to_U(l, w, 1, N)

        def outproj_residual(l, w, wo_b, wk, N, t0):
            p.dma("sp", U[:, :, :N], ymixv[:, :, t0:t0 + N], writes=UK)
            wv = wo_b.rearrange("(k p) n -> p k n", p=128)
            for cg in range(4):
                wb = wictr[0] % 2
                wictr[0] += 1
                p.dma("sp", WI[wb][:, :, 0:512], wv[:, :, cg * 512:(cg + 1) * 512], reads=wk,
                      writes=["WI%d" % wb, "WI%db" % wb])
                for j in range(4):
                    c = cg * 4 + j
                    bank = psctr[0] % 4
                    psctr[0] += 1
                    pt = PS[bank]
                    kp = "PS%d" % bank
                    for k in range(KC):
                        fl = k in (0, KC - 1)
                        p.pe(lambda e, pt=pt, wb=wb, k=k, j=j: e.matmul(
                            pt[:, :N], WI[wb][:, k, j * 128:(j + 1) * 128], U[:, k, :N], start=(k == 0), stop=(k == KC - 1)),
                            reads=(["WI%d" % wb, "WI%db" % wb] + UK) if fl else (), writes=[kp] if fl else ())
                    p.act(lambda e, pt=pt, c=c: e.activation(out=Y[:, c, :N], in_=pt[:, :N], func=AF.Copy),
                          reads=[kp], writes=["Y%d" % c])
            stats(lambda c: Y[:, c, :N], lambda c: ["Y%d" % c], N, RS2, "RS2")
            residual(l, w, 1, N)

        ffn_top = top[0]

        def rev(ap2, n):
            return bass.AP(ap2.tensor, ap2.offset + (n - 1) * ap2.ap[-1][0], [list(ap2.ap[0]), [-ap2.ap[-1][0], n]])

        def bc_chunk(ap2, pos, nch, step=64, inner=64):
            return bass.AP(ap2.tensor, ap2.offset + pos, [list(ap2.ap[0]), [step, nch], [0, inner]])

        def mixer_rg():
            top[0] = base_top
            XR = V(alloc(T), T); XC = V(alloc(T), T); XBb = V(alloc(T // 2), T // 2, BF16)
            A_ = V(alloc(T), T); BX = V(alloc(T), T); HF = V(alloc(T), T); HBk = V(alloc(T), T)
            GG = V(alloc(T // 2), T // 2, BF16); YO = V(alloc(T // 2), T // 2, BF16)
            Rr2 = [V(alloc(512), 512) for _ in range(2)]; Ii2 = [V(alloc(512), 512) for _ in range(2)]; T12 = [V(alloc(512), 512) for _ in range(2)]
            RGW = V(alloc(2048), 4096 // 2, BF16)
            sp8 = V(alloc(16), 16); spt = V(alloc(16), 16)
            p.dma("sp", HF[:, 0:4096], rgw_d[:, :], writes=["HF"])
            p.dve(lambda e: e.tensor_copy(out=RGW, in_=HF[:, 0:4096]), reads=["HF"], writes=["RGW"])
            AK = ["A_%d" % i for i in range(9)]
            BK = ["BX%d" % i for i in range(9)]
            p.act(lambda e: e.activation(out=spt, in_=pkc("rglam"), func=AF.Exp, scale=-1.0), reads=["pk"], writes=["spt"])
            p.act(lambda e: e.activation(out=spt, in_=spt, func=AF.Ln, bias=1.0, scale=1.0), reads=["spt"], writes=["spt"])
            p.dve(lambda e: e.tensor_scalar(sp8, spt, -8.0, None, ALU.mult), reads=["spt"], writes=["sp8"])
            segs = [(0, NCTX), (NCTX, T)]
            def do_cg(cg):
                p.dma("sp", XR, xrT[cg * 128:(cg + 1) * 128, :], writes=["XR"])
                p.dma("sp", GG, grT[cg * 128:(cg + 1) * 128, :], writes=["GG"])
                cw = lambda tap: pkc("convw", tap * 8 + cg, 1)
                cb = pkc("convb", cg, 1)
                for (s0, s1) in segs:
                    p.dve(lambda e, s0=s0, s1=s1: e.tensor_scalar(XC[:, s0:s1], XR[:, s0:s1], cw(2), cb, ALU.mult, ALU.add),
                          reads=["XR", "pk"], writes=["XC"])
                    for tap, off in ((0, -2), (1, -1), (3, 1)):
                        if off < 0:
                            o_ = XC[:, s0 - off:s1]; i_ = XR[:, s0:s1 + off]
                        else:
                            o_ = XC[:, s0:s1 - off]; i_ = XR[:, s0 + off:s1]
                        p.dve(lambda e, o_=o_, i_=i_, tap=tap: e.scalar_tensor_tensor(out=o_, in0=i_, scalar=cw(tap), in1=o_,
                                                                                    op0=ALU.mult, op1=ALU.add),
                              reads=["XR", "XC", "pk"], writes=["XC"])
                p.act(lambda e: e.activation(out=XBb, in_=XC, func=AF.Copy), reads=["XC"], writes=["XBb"])
                for d in range(2):
                    def do_slab(ti, t0, N):
                        bank = psctr[0] % 2
                        psctr[0] += 1
                        Rr = Rr2[bank]; Ii = Ii2[bank]; T1 = T12[bank]
                        kA = "A_%d" % ti; kB = "BX%d" % ti; kR = "Rr%d" % bank; kI = "Ii%d" % bank; kT = "T1_%d" % bank
                        pr = PS[bank * 2]; pi_ = PS[bank * 2 + 1]
                        kr = "PS%d" % (bank * 2); ki_ = "PS%d" % (bank * 2 + 1)
                        wa = RGW[:, ((0 * 2 + d) * 8 + cg) * 128:((0 * 2 + d) * 8 + cg + 1) * 128]
                        wi = RGW[:, ((1 * 2 + d) * 8 + cg) * 128:((1 * 2 + d) * 8 + cg + 1) * 128]
                        p.pe(lambda e, pr=pr, wa=wa, t0=t0, N=N: e.matmul(pr[:, :N], wa, XBb[:, t0:t0 + N], start=True, stop=True),
                             reads=["RGW", "XBb"], writes=[kr])
                        p.pe(lambda e, pi_=pi_, wi=wi, t0=t0, N=N: e.matmul(pi_[:, :N], wi, XBb[:, t0:t0 + N], start=True, stop=True),
                             reads=["RGW", "XBb"], writes=[ki_])
                        ba = pkc("rgba", d * 8 + cg, 1); bi = pkc("rgbi", d * 8 + cg, 1)
                        sc = sp8[:, d * 8 + cg:d * 8 + cg + 1]
                        p.act(lambda e, pr=pr, N=N, ba=ba: e.activation(out=Rr[:, :N], in_=pr[:, :N], func=AF.Sigmoid, bias=ba, scale=1.0),
                              reads=[kr, "pk"], writes=[kR])
                        p.act(lambda e, pi_=pi_, N=N, bi=bi: e.activation(out=Ii[:, :N], in_=pi_[:, :N], func=AF.Sigmoid, bias=bi, scale=1.0),
                              reads=[ki_, "pk"], writes=[kI])
                        p.act(lambda e, t0=t0, N=N, sc=sc: e.activation(out=A_[:, t0:t0 + N], in_=Rr[:, :N], func=AF.Exp, scale=sc),
                              reads=[kR, "sp8"], writes=[kA])
                        p.dve(lambda e, t0=t0, N=N: e.tensor_tensor(out=T1[:, :N], in0=A_[:, t0:t0 + N], in1=A_[:, t0:t0 + N], op=ALU.mult),
                              reads=[kA], writes=[kT])
                        p.act(lambda e, N=N: e.activation(out=T1[:, :N], in_=T1[:, :N], func=AF.Sqrt, bias=1.0, scale=-1.0),
                              reads=[kT], writes=[kT])
                        p.dve(lambda e, t0=t0, N=N: e.tensor_tensor(out=Ii[:, :N], in0=Ii[:, :N], in1=XC[:, t0:t0 + N], op=ALU.mult),
                              reads=[kI, "XC"], writes=[kI])
                        p.dve(lambda e, t0=t0, N=N: e.tensor_tensor(out=BX[:, t0:t0 + N], in0=Ii[:, :N], in1=T1[:, :N], op=ALU.mult),
                              reads=[kI, kT], writes=[kB])
                    for ti_, (t0_, N_, w_) in enumerate(tiles):
                        do_slab(ti_, t0_, N_)
                    if d == 0:
                        p.dve(lambda e: e.tensor_tensor_scan(HF, A_, BX, 0.0, ALU.mult, ALU.add), reads=AK + BK, writes=["HF"])
                    else:
                        p.dve(lambda e: e.tensor_tensor_scan(rev(HBk[:, 0:NCTX], NCTX), rev(A_[:, 0:NCTX], NCTX), rev(BX[:, 0:NCTX], NCTX),
                                                             0.0, ALU.mult, ALU.add), reads=AK + BK, writes=["HBk"])
                        p.dve(lambda e: e.tensor_tensor_scan(rev(HBk[:, NCTX:T], NLAT), rev(A_[:, NCTX:T], NLAT), rev(BX[:, NCTX:T], NLAT),
                                                             HBk[:, 0:1], ALU.mult, ALU.add), reads=AK + BK + ["HBk"], writes=["HBk"])
                p.dve(lambda e: e.tensor_tensor(out=HF, in0=HF, in1=HBk, op=ALU.add), reads=["HF", "HBk"], writes=["HF"])
                p.dve(lambda e: e.tensor_tensor(out=YO, in0=HF, in1=GG, op=ALU.mult), reads=["HF", "GG"], writes=["YO"])
                p.dma("pool", ymix_d[1024 + cg * 128:1024 + (cg + 1) * 128, :], YO, reads=["YO"], writes=[])

            for cg in range(8):
                do_cg(cg)

        def mixer_chunk(layer):
            top[0] = base_top
            NH = 4 if layer == 0 else 8
            PSY = PS[0][:, :].bitcast(BF16)
            DV = 256
            VW = NH * DV
            VA = DV + 1 if layer == 1 else DV
            qT_d, kT_d = (qT0, kT0) if layer == 0 else (qT1, kT1)
            v_d = v0 if layer == 0 else v1
            gate_d = sg0 if layer == 0 else so1
            ob_d = ob0 if layer == 0 else hb1
            MF = V(alloc(64), 64); MB = V(alloc(64), 64)
            RM = V(alloc(512), 512)
            GN = V(alloc(VW), VW)
            Qs = [V(alloc(256), 256, BF16) for _ in range(NH)]
            Ks = [V(alloc(256), 256, BF16) for _ in range(NH)]
            Vt = V(alloc(8 * NH * VA // 2 + 8), 8 * NH * VA // 2 + 8, BF16)[:, 0:8 * NH * VA].rearrange("p (c h v) -> p c h v", c=8, h=NH)
            S32 = V(alloc(NH * VA), NH * VA, F32, "p (h v) -> p h v", h=NH)
            Sb2 = [V(alloc(NH * VA // 2 + 4), NH * VA // 2 + 4, BF16)[:, 0:NH * VA].rearrange("p (h v) -> p h v", h=NH) for _ in range(2)]
            STM = [V(alloc(NH * 32), NH * 32, BF16, "p (h i) -> p h i", h=NH) for _ in range(2)]
            NB = 2 if layer == 0 else 1
            OT = [V(alloc(VW), VW) for _ in range(2)]
            OBc = [V(alloc(VW), VW) for _ in range(NB)] * (3 - NB)
            SGc = [V(alloc(VW // 2), VW // 2, BF16) for _ in range(NB)] * (3 - NB)
            T1 = V(alloc(VW), VW)
            YG = [V(alloc(VW // 2), VW // 2, BF16) for _ in range(NB)] * (3 - NB)
            JK = V(alloc(128), 128, BF16)
            SS = V(alloc(NH), NH); RSg = V(alloc(NH), NH)
            YM = V(alloc(VW // 128 * 256), VW // 128 * 256, BF16, "p (c n) -> p c n", n=512)
            p.pool(lambda e: e.memset(MF, 1.0), writes=["MF"])
            p.pool(lambda e: e.affine_select(out=MF[0:64, :], in_=MF[0:64, :], pattern=[[1, 64]], compare_op=ALU.is_ge, fill=0.0,
                                             base=0, channel_multiplier=-1), reads=["MF"], writes=["MF"])
            p.pool(lambda e: e.memset(MB, 1.0), writes=["MB"])
            p.pool(lambda e: e.affine_select(out=MB[0:64, :], in_=MB[0:64, :], pattern=[[-1, 64]], compare_op=ALU.is_ge, fill=0.0,
                                             base=0, channel_multiplier=1), reads=["MB"], writes=["MB"])
            p.dma("sp", GN, (gn4_d if layer == 0 else mn8_d)[:, :], writes=["GN"])
            if layer == 0:
                WA2 = V(alloc(1024), 1024)
                LR = V(alloc(512), 512)
                p.dma("sp", WA2[0:16, :], wa2_d[:, :], writes=["WA2"])
                p.dve(lambda e: e.memset(RM, 1.0), writes=["RM"])
                p.dve(lambda e: e.memset(RM.rearrange("p (c j) -> p c j", j=64)[:, :, 0:1], 0.0), reads=["RM"], writes=["RM"])
                ZB = [V(alloc(512), 512) for _ in range(NH)]
                BB = [V(alloc(512), 512) for _ in range(NH)]
                D1 = [V(alloc(512), 512) for _ in range(2)]
                EE = [V(alloc(512), 512) for _ in range(2)]
                E3 = [V(alloc(512), 512) for _ in range(NH)]
                QSs = [V(alloc(256), 256, BF16) for _ in range(NH)]
                KSs = [V(alloc(256), 256, BF16) for _ in range(NH)]
                QEs = [V(alloc(256), 256, BF16) for _ in range(NH)]
                KDs = [V(alloc(256), 256, BF16) for _ in range(NH)]
                KDT = [V(alloc(256), 256, BF16) for _ in range(2)]
            else:
                p.dve(lambda e: e.memset(RM, 1.0), writes=["RM"])
                p.dve(lambda e: e.memset(RM.rearrange("p (c j) -> p c j", j=64)[:, :, 0:1], 0.0), reads=["RM"], writes=["RM"])
                RMN = V(alloc(512), 512)
                p.dve(lambda e: e.memset(RMN, 0.0), writes=["RMN"])
                p.dve(lambda e: e.memset(RMN.rearrange("p (c j) -> p c j", j=64)[:, :, 0:1], -1e30), reads=["RMN"], writes=["RMN"])
                I8 = V(alloc(512), 512); F8 = V(alloc(512), 512); B8 = V(alloc(512), 512); GK = V(alloc(512), 512)
                PM = V(alloc(512), 512); WK8 = V(alloc(512), 512); EB8 = V(alloc(512), 512)
                MN = V(alloc(8), 8); MC = V(alloc(8), 8); RC = V(alloc(8), 8); DC = V(alloc(8), 8)
                MCAR = V(alloc(1), 1)
                DEXP = V(alloc(64), 64)
                ONES32 = V(alloc(128), 128)
                WKT = V(alloc(64), 64); EBT = V(alloc(64), 64)
                DECB = V(alloc(64), 64, F32, "p (h c) -> p h c", h=8)
                KTMc = [V(alloc(512), 512, BF16) for _ in range(2)]
                KTS = [V(alloc(512), 512, BF16) for _ in range(2)]
                TMPS = [V(alloc(512), 512) for _ in range(2)]
                DEN = V(alloc(8), 8); RDN = V(alloc(8), 8)
                p.dve(lambda e: e.memset(ONES32, 1.0), writes=["ONES32"])
                p.dve(lambda e: e.memset(Vt[:, :, :, 256:257], 1.0), writes=["Vt"])

            def do_dir(d):
                fwd = d == 0
                mask = MF if fwd else MB
                mk = "MF" if fwd else "MB"
                p.dve(lambda e: e.memset(S32, 0.0), writes=["S32"])
                p.dve(lambda e: e.memset(Sb2[0], 0.0), writes=["Sb_0"] + ["Sb%d_0" % h for h in range(NH)])
                p.dve(lambda e: e.memset(Sb2[1], 0.0), writes=["Sb_1"] + ["Sb%d_1" % h for h in range(NH)])
                if layer == 1:
                    p.dve(lambda e: e.memset(MCAR[0:8, :], 0.0), writes=["MCAR"])
                order = list(range(9)) if fwd else [0] + list(range(8, 0, -1))
                def slab(ti):
                    t0, N, w = tiles[ti]
                    nch = N // 64
                    readout = not (layer == 1 and w == 1)
                    for h in range(NH):
                        p.dma("sp", Qs[h][:, :N], qT_d[h * 128:(h + 1) * 128, t0:t0 + N], writes=["Q%d" % h])
                        p.dma("sp", Ks[h][:, :N], kT_d[h * 128:(h + 1) * 128, t0:t0 + N], writes=["K%d" % h])
                    vsrc = v_d[t0:t0 + N, :].rearrange("(c j) (h v) -> j c h v", j=64, v=DV)
                    if layer == 0:
                        p.dma("sp", Vt[0:64, 0:nch, 0:4, 0:DV], vsrc[:, :, 0:4, :], writes=["Vt"])
                    else:
                        for c_ in range(nch):
                            p.dma("sp", Vt[0:64, c_, :, 0:DV], vsrc[:, c_, :, :], writes=["Vt"])
                    first_pos = 0 if fwd else 63
                    last_pos = 63 if fwd else 0
                    if layer == 0:
                        p.dma("sp", LR[0:16, :N], lrT[d * 16:(d + 1) * 16, t0:t0 + N], writes=["LR"])
                        for h in range(NH):
                            pz = PS[h]
                            kz = "PS%d" % h
                            p.pe(lambda e, pz=pz, h=h: e.matmul(pz[:, :N], WA2[0:16, d * 512 + h * 128:d * 512 + (h + 1) * 128], LR[0:16, :N],
                                                                start=True, stop=True), reads=["WA2", "LR"], writes=[kz])
                            bal = pkc("balpha", d * 4 + h, 1)
                            p.dve(lambda e, pz=pz, h=h, bal=bal: e.tensor_scalar(ZB[h][:, :N], pz[:, :N], bal, -20.0, ALU.add, ALU.max),
                                  reads=[kz, "pk"], writes=["ZB%d" % h])
                            p.act(lambda e, h=h: e.activation(out=ZB[h][:, :N], in_=ZB[h][:, :N], func=AF.Exp, scale=-1.0),
                                  reads=["ZB%d" % h], writes=["ZB%d" % h])
                            p.act(lambda e, h=h: e.activation(out=ZB[h][:, :N], in_=ZB[h][:, :N], func=AF.Ln, bias=1.0, scale=1.0),
                                  reads=["ZB%d" % h], writes=["ZB%d" % h])
                            p.dve(lambda e, h=h: e.tensor_scalar(ZB[h][:, :N], ZB[h][:, :N], -1.0 / 16.0, -1.0, ALU.mult, ALU.max),
                                  reads=["ZB%d" % h], writes=["ZB%d" % h])
                            if fwd:
                                p.dve(lambda e, h=h: e.tensor_tensor_scan(BB[h][:, :N], RM[:, :N], ZB[h][:, :N], 0.0, ALU.mult, ALU.add),
                                      reads=["ZB%d" % h, "RM"], writes=["BB%d" % h])
                            else:
                                p.dve(lambda e, h=h: e.tensor_tensor_scan(rev(BB[h][:, :N], N), RM[:, :N], rev(ZB[h][:, :N], N), 0.0,
                                                                          ALU.mult, ALU.add),
                                      reads=["ZB%d" % h, "RM"], writes=["BB%d" % h])
                            ref_pos = 32 if fwd else 31
                            b3 = BB[h][:, :N].rearrange("p (c j) -> p c j", j=64)
                            dd = D1[h % 2]; dk_ = "D1%d" % (h % 2)
                            ea = EE[0]; eb_ = EE[1]
                            p.dve(lambda e, h=h, b3=b3, dd=dd: e.tensor_tensor(out=dd[:, :N].rearrange("p (c j) -> p c j", j=64), in0=b3,
                                                                               in1=bc_chunk(BB[h], ref_pos, nch), op=ALU.subtract),
                                  reads=["BB%d" % h], writes=[dk_])
                            p.act(lambda e, dd=dd, ea=ea: e.activation(out=ea[:, :N], in_=dd[:, :N], func=AF.Exp), reads=[dk_], writes=["EE0"])
                            p.pool(lambda e, h=h, ea=ea: e.tensor_tensor(out=QSs[h][:, :N], in0=Qs[h][:, :N], in1=ea[:, :N], op=ALU.mult),
                                  reads=["EE0", "Q%d" % h], writes=["QS%d" % h])
                            p.act(lambda e, dd=dd, eb_=eb_: e.activation(out=eb_[:, :N], in_=dd[:, :N], func=AF.Exp, scale=-1.0),
                                  reads=[dk_], writes=["EE1"])
                            p.pool(lambda e, h=h, eb_=eb_: e.tensor_tensor(out=KSs[h][:, :N], in0=Ks[h][:, :N], in1=eb_[:, :N], op=ALU.mult),
                                  reads=["EE1", "K%d" % h], writes=["KS%d" % h])
                            p.act(lambda e, h=h: e.activation(out=E3[h][:, :N], in_=BB[h][:, :N], func=AF.Exp), reads=["BB%d" % h],
                                  writes=["E3%d" % h])
                            p.pool(lambda e, h=h: e.tensor_tensor(out=QEs[h][:, :N], in0=Qs[h][:, :N], in1=E3[h][:, :N], op=ALU.mult),
                                  reads=["E3%d" % h, "Q%d" % h], writes=["QE%d" % h])
                            p.dve(lambda e, h=h, b3=b3, dd=dd: e.tensor_tensor(out=dd[:, :N].rearrange("p (c j) -> p c j", j=64),
                                                                               in0=bc_chunk(BB[h], last_pos, nch), in1=b3, op=ALU.subtract),
                                  reads=["BB%d" % h], writes=[dk_])
                            p.act(lambda e, dd=dd, ea=ea: e.activation(out=ea[:, :N], in_=dd[:, :N], func=AF.Exp), reads=[dk_], writes=["EE0"])
                            p.pool(lambda e, h=h, ea=ea: e.tensor_tensor(out=KDs[h][:, :N], in0=Ks[h][:, :N], in1=ea[:, :N], op=ALU.mult),
                                  reads=["EE0", "K%d" % h], writes=["KD%d" % h])
                    else:
                        p.dma("sp", I8[0:8, :N], gT1[d * 16:d * 16 + 8, t0:t0 + N], writes=["I8"])
                        p.dma("sp", F8[0:8, :N], gT1[d * 16 + 8:d * 16 + 16, t0:t0 + N], writes=["F8"])
                        p.act(lambda e: e.activation(out=F8[0:8, :N], in_=F8[0:8, :N], func=AF.Exp, scale=-1.0), reads=["F8"], writes=["F8"])
                        p.act(lambda e: e.activation(out=F8[0:8, :N], in_=F8[0:8, :N], func=AF.Ln, bias=1.0, scale=1.0), reads=["F8"], writes=["F8"])
                        p.dve(lambda e: e.tensor_scalar(F8[0:8, :N], F8[0:8, :N], -1.0, None, ALU.mult), reads=["F8"], writes=["F8"])
                        if fwd:
                            p.dve(lambda e: e.tensor_tensor_scan(B8[0:8, :N], RM[0:8, :N], F8[0:8, :N], 0.0, ALU.mult, ALU.add),
                                  reads=["F8", "RM"], writes=["B8"])
                        else:
                            p.dve(lambda e: e.tensor_tensor_scan(rev(B8[0:8, :N], N), RM[0:8, :N], rev(F8[0:8, :N], N), 0.0, ALU.mult, ALU.add),
                                  reads=["F8", "RM"], writes=["B8"])
                        p.dve(lambda e: e.tensor_tensor(out=GK[0:8, :N], in0=I8[0:8, :N], in1=B8[0:8, :N], op=ALU.subtract),
                              reads=["I8", "B8"], writes=["GK"])
                        if fwd:
                            p.dve(lambda e: e.tensor_tensor_scan(PM[0:8, :N], RMN[0:8, :N], GK[0:8, :N], -1e30, ALU.add, ALU.max),
                                  reads=["GK", "RMN"], writes=["PM"])
                        else:
                            p.dve(lambda e: e.tensor_tensor_scan(rev(PM[0:8, :N], N), RMN[0:8, :N], rev(GK[0:8, :N], N), -1e30, ALU.add, ALU.max),
                                  reads=["GK", "RMN"], writes=["PM"])
                        def strided(ap2, pos):
                            return bass.AP(ap2.tensor, ap2.offset + pos, [[ap2.ap[0][0], 8], [64, nch]])
                        pml = strided(PM, last_pos); bl = strided(B8, last_pos)
                        mn = MN[0:8, 0:nch]; mc = MC[0:8, 0:nch]
                        if fwd:
                            p.dve(lambda e: e.tensor_tensor_scan(mn, pml, bl, MCAR[0:8, :], ALU.max, ALU.add),
                                  reads=["PM", "B8", "MCAR"], writes=["MN"])
                            p.dve(lambda e: e.tensor_copy(out=MC[0:8, 0:1], in_=MCAR[0:8, :]), reads=["MCAR"], writes=["MC"])
                            if nch > 1:
                                p.dve(lambda e: e.tensor_copy(out=MC[0:8, 1:nch], in_=MN[0:8, 0:nch - 1]), reads=["MN", "MC"], writes=["MC"])
                            p.dve(lambda e: e.tensor_copy(out=MCAR[0:8, :], in_=MN[0:8, nch - 1:nch]), reads=["MN", "MC"], writes=["MCAR"])
                        else:
                            p.dve(lambda e: e.tensor_tensor_scan(rev(mn, nch), rev(pml, nch), rev(bl, nch), MCAR[0:8, :], ALU.max, ALU.add),
                                  reads=["PM", "B8", "MCAR"], writes=["MN"])
                            p.dve(lambda e: e.tensor_copy(out=MC[0:8, nch - 1:nch], in_=MCAR[0:8, :]), reads=["MCAR"], writes=["MC"])
                            if nch > 1:
                                p.dve(lambda e: e.tensor_copy(out=MC[0:8, 0:nch - 1], in_=MN[0:8, 1:nch]), reads=["MN", "MC"], writes=["MC"])
                            p.dve(lambda e: e.tensor_copy(out=MCAR[0:8, :], in_=MN[0:8, 0:1]), reads=["MN", "MC"], writes=["MCAR"])
                        p.dve(lambda e: e.tensor_tensor(out=RC[0:8, 0:nch], in0=mn, in1=bl, op=ALU.subtract), reads=["MN", "B8"], writes=["RC"])
                        p.dve(lambda e: e.tensor_tensor(out=DC[0:8, 0:nch], in0=mc, in1=RC[0:8, 0:nch], op=ALU.subtract),
                              reads=["MC", "RC"], writes=["DC"])
                        p.act(lambda e: e.activation(out=DC[0:8, 0:nch], in_=DC[0:8, 0:nch], func=AF.Exp), reads=["DC"], writes=["DC"])
                        rcb = bass.AP(RC.tensor, RC.offset, [[RC.ap[0][0], 8], [1, nch], [0, 64]])
                        p.dve(lambda e: e.tensor_tensor(out=WK8[0:8, :N].rearrange("p (c j) -> p c j", j=64),
                                                        in0=GK[0:8, :N].rearrange("p (c j) -> p c j", j=64), in1=rcb, op=ALU.subtract),
                              reads=["GK", "RC"], writes=["WK8"])
                        p.act(lambda e: e.activation(out=WK8[0:8, :N], in_=WK8[0:8, :N], func=AF.Exp), reads=["WK8"], writes=["WK8"])
                        p.dve(lambda e: e.tensor_tensor(out=EB8[0:8, :N].rearrange("p (c j) -> p c j", j=64),
                                                        in0=B8[0:8, :N].rearrange("p (c j) -> p c j", j=64), in1=rcb, op=ALU.add),
                              reads=["B8", "RC"], writes=["EB8"])
                        p.act(lambda e: e.activation(out=EB8[0:8, :N], in_=EB8[0:8, :N], func=AF.Exp, scale=-1.0), reads=["EB8"], writes=["EB8"])
                        ptw = PS[0]; pte = PS[1]; pdc = PS[2]
                        for c in range(nch):
                            p.pe(lambda e, c=c: e.transpose(ptw[0:64, c * 8:(c + 1) * 8], WK8[0:8, c * 64:(c + 1) * 64], identf[0:8, 0:8]),
                                 reads=["WK8", "identf"], writes=["PS0"])
                            p.pe(lambda e, c=c: e.transpose(pte[0:64, c * 8:(c + 1) * 8], EB8[0:8, c * 64:(c + 1) * 64], identf[0:8, 0:8]),
                                 reads=["EB8", "identf"], writes=["PS1"])
                        p.dve(lambda e: e.tensor_copy(out=WKT[0:64, 0:nch * 8], in_=ptw[0:64, 0:nch * 8]), reads=["PS0"], writes=["WKT"])
                        p.dve(lambda e: e.tensor_copy(out=EBT[0:64, 0:nch * 8], in_=pte[0:64, 0:nch * 8]), reads=["PS1"], writes=["EBT"])
                        dcb = bass.AP(DC.tensor, DC.offset, [[DC.ap[0][0], 8], [0, 8], [1, nch]])
                        idb = bass.AP(identf.tensor, identf.offset, [[identf.ap[0][0], 8], [1, 8], [0, nch]])
                        p.dve(lambda e: e.tensor_tensor(out=DEXP[0:8, 0:8 * nch].rearrange("p (h c) -> p h c", h=8), in0=dcb, in1=idb, op=ALU.mult),
                              reads=["DC", "identf"], writes=["DEXP"])
                        p.pe(lambda e: e.matmul(pdc[:, 0:8 * nch], ONES32[0:8, :], DEXP[0:8, 0:8 * nch], start=True, stop=True),
                             reads=["ONES32", "DEXP"], writes=["PS2"])
                        p.dve(lambda e: e.tensor_copy(out=DECB[:, :, 0:nch], in_=pdc[:, 0:8 * nch].rearrange("p (h c) -> p h c", h=8)),
                              reads=["PS2"], writes=["DECB"])

                    corder = range(nch) if fwd else range(nch - 1, -1, -1)
                    def chunk(ci, c):
                        cs = slice(c * 64, (c + 1) * 64)
                        tg = t0 + c * 64
                        sb_ = ci % 2
                        stm = STM[sb_]
                        Sb = Sb2[sb_]
                        Sbn = Sb2[1 - sb_]
                        kst = "STM%d" % sb_
                        pst = PS[4]
                        for h in range(NH):
                            lq = QSs[h] if layer == 0 else Qs[h]
                            lk = KSs[h] if layer == 0 else Ks[h]
                            rk_ = ["QS%d" % h, "KS%d" % h] if layer == 0 else ["Q%d" % h, "K%d" % h]
                            p.pe(lambda e, h=h, lq=lq, lk=lk, cs=cs: e.matmul(pst[0:64, h * 64:(h + 1) * 64], lk[:, cs], lq[:, cs],
                                                                              start=True, stop=True), reads=rk_, writes=["PS4"])
                        pv = pst[0:64, 0:NH * 64].rearrange("p (h i) -> p h i", h=NH)
                        mb_ = bass.AP(mask.tensor, mask.offset, [[mask.ap[0][0], 64], [0, NH], [1, 64]])
                        if layer == 0:
                            p.dve(lambda e, stm=stm, pv=pv, mb_=mb_: e.tensor_tensor(out=stm[0:64, :, :], in0=pv, in1=mb_, op=ALU.mult),
                                  reads=["PS4", mk], writes=[kst])
                        else:
                            wkb = bass.AP(WKT.tensor, WKT.offset + c * 8, [[WKT.ap[0][0], 64], [1, NH], [0, 64]])
                            tmps = TMPS[sb_][0:64, :].rearrange("p (h i) -> p h i", h=NH)
                            p.dve(lambda e, pv=pv, wkb=wkb, tmps=tmps: e.tensor_tensor(out=tmps, in0=pv, in1=wkb, op=ALU.mult),
                                  reads=["PS4", "WKT"], writes=["TMPS%d" % sb_])
                            p.pool(lambda e, stm=stm, tmps=tmps, mb_=mb_: e.tensor_tensor(out=stm[0:64, :, :], in0=tmps, in1=mb_, op=ALU.mult),
                                  reads=["TMPS%d" % sb_, mk], writes=[kst])
                        if layer == 0:
                            kdt = KDT[sb_]
                            for h in range(NH):
                                p.pe(lambda e, h=h, cs=cs: e.transpose(PSB[0:64, h * 128:(h + 1) * 128], KDs[h][:, cs], identb),
                                     reads=["KD%d" % h, "identb"], writes=["PSBa"])
                            p.act(lambda e, kdt=kdt: e.activation(out=kdt[0:64, 0:512], in_=PSB[0:64, 0:512], func=AF.Copy),
                                  reads=["PSBa"], writes=["KDT%d" % sb_])
                        else:
                            ktm = KTMc[sb_]
                            kts = KTS[sb_]
                            p.dma("sp", ktm[0:64, :], ktm1[tg:tg + 64, :], writes=["KTM%d" % sb_])
                            wkk = bass.AP(WKT.tensor, WKT.offset + c * 8, [[WKT.ap[0][0], 64], [1, NH], [0, 128]])
                            p.pool(lambda e, ktm=ktm, kts=kts, wkk=wkk: e.tensor_tensor(
                                out=kts[0:64, :].rearrange("p (h d) -> p h d", h=NH), in0=ktm[0:64, :].rearrange("p (h d) -> p h d", h=NH),
                                in1=wkk, op=ALU.mult), reads=["KTM%d" % sb_, "WKT"], writes=["KTS%d" % sb_])
                            for h in range(NH):
                                p.act(lambda e, h=h, c=c, Sb=Sb: e.activation(out=Sb[:, h, :], in_=S32[:, h, :], func=AF.Copy, scale=DECB[:, h, c:c + 1]),
                                      reads=["S32_%d" % h, "S32", "DECB"], writes=["Sb%d_%d" % (h, sb_)])
                        if layer == 0:
                            for h in range(NH):
                                pu = PS[2 + h // 2]
                                kpu = "PS%d" % (2 + h // 2)
                                usl = pu[:, (h % 2) * 256:(h % 2 + 1) * 256]
                                p.pe(lambda e, h=h, usl=usl, kdt=kdt, c=c: e.matmul(usl, kdt[0:64, h * 128:(h + 1) * 128], Vt[0:64, c, h, :], start=True, stop=True),
                                     reads=["KDT%d" % sb_, "Vt"], writes=[kpu])
                                dsc = E3[h][:, c * 64 + last_pos:c * 64 + last_pos + 1]
                                p.dve(lambda e, h=h, usl=usl, dsc=dsc: e.scalar_tensor_tensor(out=S32[:, h, :], in0=S32[:, h, :], scalar=dsc, in1=usl,
                                                                                             op0=ALU.mult, op1=ALU.add),
                                      reads=[kpu, "E3%d" % h, "S32"], writes=["S32_%d" % h])
                            p.act(lambda e, Sbn=Sbn: e.activation(out=Sbn, in_=S32, func=AF.Copy), reads=["S32_%d" % h for h in range(NH)] + ["S32"],
                                  writes=["Sb_%d" % (1 - sb_)])
                        else:
                            for h in range(NH):
                                pu = PS[2 + h % 2]
                                kpu = "PS%d" % (2 + h % 2)
                                usl = pu[:, 0:VA]
                                p.pe(lambda e, h=h, usl=usl, kts=kts, c=c: e.matmul(usl, kts[0:64, h * 128:(h + 1) * 128], Vt[0:64, c, h, :], start=True, stop=True),
                                     reads=["KTS%d" % sb_, "Vt"], writes=[kpu])
                                p.dve(lambda e, h=h, usl=usl, c=c: e.scalar_tensor_tensor(out=S32[:, h, :], in0=S32[:, h, :], scalar=DECB[:, h, c:c + 1], in1=usl,
                                                                                          op0=ALU.mult, op1=ALU.add),
                                      reads=[kpu, "DECB", "S32", "Sb%d_%d" % (h, sb_)], writes=["S32_%d" % h])
                        post = None
                        if readout:
                            ot = OT[sb_]
                            kot = "OT%d" % sb_
                            if layer == 0:
                                for h in range(NH):
                                    po = PS[5 + h // 2]
                                    kpo = "PS%d" % (5 + h // 2)
                                    osl = po[0:64, (h % 2) * 256:(h % 2 + 1) * 256]
                                    p.pe(lambda e, h=h, osl=osl, stm=stm, c=c: e.matmul(osl, stm[0:64, h, :], Vt[0:64, c, h, :], start=True, stop=False),
                                         reads=[kst, "Vt"], writes=[kpo])
                                    p.pe(lambda e, h=h, osl=osl, cs=cs, Sb=Sb: e.matmul(osl, QEs[h][:, cs], Sb[:, h, :], start=False, stop=True),
                                         reads=["QE%d" % h, "Sb_%d" % sb_], writes=[kpo])
                                if not fwd:
                                    p.act(lambda e, ot=ot: e.activation(out=ot[0:64, 0:512], in_=PS[5][0:64, :], func=AF.Copy), reads=["PS5"], writes=[kot])
                                    p.dve(lambda e, ot=ot: e.tensor_copy(out=ot[0:64, 512:1024], in_=PS[6][0:64, :]), reads=["PS6"], writes=[kot + "b"])
                                else:
                                    obc = OBc[sb_]
                                    p.dma("sp", obc[0:64, :], ob_d[tg:tg + 64, :], reads=["sc_ob%d" % tg], writes=["OBc%d" % (sb_ % NB)])
                                    p.dve(lambda e, ot=ot, obc=obc: e.tensor_tensor(out=ot[0:64, 0:512], in0=PS[5][0:64, :], in1=obc[0:64, 0:512], op=ALU.add),
                                          reads=["PS5", "OBc%d" % (sb_ % NB)], writes=[kot])
                                    p.dve(lambda e, ot=ot, obc=obc: e.tensor_tensor(out=ot[0:64, 512:1024], in0=PS[6][0:64, :], in1=obc[0:64, 512:1024], op=ALU.add),
                                          reads=["PS6", "OBc%d" % (sb_ % NB)], writes=[kot + "b"])
                            else:
                                if fwd:
                                    obc = OBc[sb_]
                                    p.dma("sp", obc[0:64, :], ob_d[tg:tg + 64, :], reads=["sc_ob%d" % tg], writes=["OBc%d" % (sb_ % NB)])
                                for h in range(NH):
                                    po = PS[5 + h % 2]
                                    kpo = "PS%d" % (5 + h % 2)
                                    osl = po[0:64, 0:VA]
                                    p.pe(lambda e, h=h, osl=osl, stm=stm, c=c: e.matmul(osl, stm[0:64, h, :], Vt[0:64, c, h, :], start=True, stop=False),
                                         reads=[kst, "Vt"], writes=[kpo])
                                    p.pe(lambda e, h=h, osl=osl, cs=cs, Sb=Sb: e.matmul(osl, Qs[h][:, cs], Sb[:, h, :], start=False, stop=True),
                                         reads=["Q%d" % h, "Sb%d_%d" % (h, sb_)], writes=[kpo])
                                    p.dve(lambda e, h=h, po=po: e.tensor_scalar(DEN[0:64, h:h + 1], po[0:64, 256:257], -1.0, None, ALU.mult),
                                          reads=[kpo], writes=["DEN%d" % h])
                                    p.dve(lambda e, h=h, po=po, c=c: e.scalar_tensor_tensor(
                                        out=DEN[0:64, h:h + 1], in0=DEN[0:64, h:h + 1], scalar=EBT[0:64, c * 8 + h:c * 8 + h + 1],
                                        in1=po[0:64, 256:257], op0=ALU.max, op1=ALU.max),
                                          reads=[kpo, "EBT", "DEN%d" % h], writes=["DEN%d" % h])
                                    p.dve(lambda e, h=h: e.reciprocal(RDN[0:64, h:h + 1], DEN[0:64, h:h + 1]), reads=["DEN%d" % h], writes=["RDN%d" % h])
                                    if fwd:
                                        p.dve(lambda e, h=h, po=po, ot=ot, obc=obc: e.scalar_tensor_tensor(
                                            out=ot[0:64, h * 256:(h + 1) * 256], in0=po[0:64, 0:256], scalar=RDN[0:64, h:h + 1],
                                            in1=obc[0:64, h * 256:(h + 1) * 256], op0=ALU.mult, op1=ALU.add),
                                            reads=[kpo, "RDN%d" % h, "OBc%d" % (sb_ % NB)], writes=[kot + "_%d" % h])
                                    else:
                                        p.act(lambda e, h=h, po=po, ot=ot: e.activation(out=ot[0:64, h * 256:(h + 1) * 256], in_=po[0:64, 0:256],
                                                                                        func=AF.Copy, scale=RDN[0:64, h:h + 1]),
                                              reads=[kpo, "RDN%d" % h], writes=[kot + "_%d" % h])
                            okeys = [kot, kot + "b"] if layer == 0 else [kot + "_%d" % h for h in range(NH)]
                            def post():
                                if not fwd:
                                    p.dma("pool", ob_d[tg:tg + 64, :], ot[0:64, :], reads=okeys, writes=["sc_ob%d" % tg])
                                else:
                                    sgc = SGc[sb_]
                                    p.dma("sp", sgc[0:64, :], gate_d[tg:tg + 64, :], writes=["SGc%d" % (sb_ % NB)])
                                    for h in range(NH):
                                        p.act(lambda e, h=h, ot=ot: e.activation(out=JK[0:64, 0:256], in_=ot[0:64, h * 256:(h + 1) * 256], func=AF.Square,
                                                                                 accum_out=SS[0:64, h:h + 1]), reads=okeys, writes=["JK", "SS%d" % h])
                                    p.act(lambda e: e.activation(out=SS[0:64, :], in_=SS[0:64, :], func=AF.Sqrt, bias=scr[0:64, 4:5], scale=1.0 / DV),
                                          reads=["SS%d" % h for h in range(NH)] + ["eps"], writes=["SSq"])
                                    p.dve(lambda e: e.reciprocal(RSg[0:64, :], SS[0:64, :]), reads=["SSq"], writes=["RSg"] + ["SS%d" % h for h in range(NH)])
                                    p.pool(lambda e, sgc=sgc: e.tensor_tensor(out=T1[0:64, :], in0=sgc[0:64, :], in1=GN[0:64, :], op=ALU.mult),
                                          reads=["SGc%d" % (sb_ % NB), "GN"], writes=["T1"])
                                    yg = YG[sb_]
                                    for h in range(NH):
                                        p.dve(lambda e, h=h, ot=ot, yg=yg: e.scalar_tensor_tensor(
                                            out=yg[0:64, h * 256:(h + 1) * 256], in0=ot[0:64, h * 256:(h + 1) * 256], scalar=RSg[0:64, h:h + 1],
                                            in1=T1[0:64, h * 256:(h + 1) * 256], op0=ALU.mult, op1=ALU.mult),
                                            reads=okeys + ["RSg", "T1"], writes=["YG%d_%d" % (sb_ % NB, h)])
                                    nblk = VW // 128
                                    for j0 in range(0, nblk, 8):
                                        for j in range(j0, j0 + 8):
                                            p.pe(lambda e, j=j, j0=j0, yg=yg: e.transpose(PSY[:, (j - j0) * 64:(j - j0 + 1) * 64],
                                                                                         yg[0:64, j * 128:(j + 1) * 128], identb[0:64, 0:64]),
                                                 reads=["YG%d_%d" % (sb_ % NB, j // 2), "identb"], writes=["PS0"])
                                        p.act(lambda e, j0=j0, c=c: e.activation(out=YM[:, j0:j0 + 8, c * 64:(c + 1) * 64],
                                                                                  in_=PSY[:, 0:512].rearrange("p (j t) -> p j t", t=64), func=AF.Copy),
                                              reads=["PS0"], writes=["YM"])
                        return post

                    pending = None
                    for ci, c in enumerate(corder):
                        nxt = chunk(ci, c)
                        if pending is not None:
                            pending()
                        pending = nxt
                    if pending is not None:
                        pending()
                    if fwd and readout:
                        nb = VW // 128
                        p.dma("pool", ymixv[:, 0:nb, t0:t0 + N], YM[:, 0:nb, :N], reads=["YM"], writes=[])

                for ti in order:
                    slab(ti)

            for d in (1, 0):
                do_dir(d)

        full = STAGE == "full"
        for ti in range(len(tiles)):
            t0, N, w = tiles[ti]
            cur["N"] = N; cur["t0"] = t0
            load_x_tile(ti)
            ffn(0, w, 0, f1in_b[0], f1out_b[0], wkeys["f1in0"], wkeys["f1out0"], N)
            if STAGE == "ffn1":
                if w == 0:
                    store_out_tile(ti)
                continue
            premix(0, w, N)
            inproj0()
            store_h(ti)
        if STAGE != "ffn1":
            barrier()
            conv_B()
            mixer_rg()
            barrier()
            mixer_chunk(0)
            barrier()
            for ti in range(len(tiles)):
                t0, N, w = tiles[ti]
                cur["N"] = N; cur["t0"] = t0
                load_h(ti)
                outproj_residual(0, w, about_b, wkeys["about"], N, t0)
                if STAGE == "mix0":
                    if w == 0:
                        store_out_tile(ti)
                    continue
                ffn(0, w, 2, f2in_b[0], f2out_b[0], wkeys["f2in0"], wkeys["f2out0"], N)
                if STAGE == "l0":
                    if w == 0:
                        store_out_tile(ti)
                    continue
                ffn(1, w, 0, f1in_b[1], f1out_b[1], wkeys["f1in1"], wkeys["f1out1"], N)
                premix(1, w, N)
                inproj1()
                if w == 0:
                    store_h(ti)
        if full:
            barrier()
            conv_C()
            mixer_chunk(1)
            barrier()
            for ti in range(1, len(tiles)):
                t0, N, w = tiles[ti]
                cur["N"] = N; cur["t0"] = t0
                load_h(ti)
                outproj_residual(1, w, mlout_b, wkeys["mlout"], N, t0)
                ffn(1, w, 2, f2in_b[1], f2out_b[1], wkeys["f2in1"], wkeys["f2out1"], N)
                store_out_tile(ti)
        p.emit(final_waits=("sp", "pool"))
    return nc


def make_in_maps(inputs):
    g = lambda k: np.asarray(inputs[k], dtype=np.float32)
    x, c, ctx, c_ctx = g("x"), g("c"), g("ctx"), g("c_ctx")
    shared = {
        "w_mod": np.ascontiguousarray(g("w_mod")),
        "ffn1_w_in": np.ascontiguousarray(g("ffn1_w_in")), "ffn1_w_out": np.ascontiguousarray(g("ffn1_w_out")),
        "ffn2_w_in": np.ascontiguousarray(g("ffn2_w_in")), "ffn2_w_out": np.ascontiguousarray(g("ffn2_w_out")),
        "ab_w_in": np.ascontiguousarray(g("ab_w_in")[0]), "ab_w_out": np.ascontiguousarray(g("ab_w_out")[0]),
        "ml_w_in": np.ascontiguousarray(g("ml_w_in")[0]), "ml_w_out": np.ascontiguousarray(g("ml_w_out")[0]),
    }
    shared["wa2"] = np.ascontiguousarray(np.concatenate([g("gla_w_alpha2")[0, 0], g("gla_w_alpha2")[0, 1]], axis=1))
    rw = np.stack([g("rg_w_a")[0], g("rg_w_i")[0]], axis=0)
    shared["rgw"] = np.ascontiguousarray(np.transpose(rw, (3, 0, 1, 2, 4)).reshape(128, 4096))
    shared["gn4"] = np.ascontiguousarray(np.broadcast_to(np.tile(g("gla_norm_g")[0], 4)[None, :], (128, 1024)))
    shared["mn8"] = np.ascontiguousarray(np.broadcast_to(np.tile(g("ml_norm_g")[0], 8)[None, :], (128, 2048)))
    mlbg = np.zeros((128, 1), np.float32)
    mlbg[:32, 0] = g("ml_b_gates")[0]
    in_maps = []
    for b in range(8):
        pk = np.concatenate([
            fm(c[b]), fm(c_ctx), fm(g("b_mod").reshape(2, 9 * D)), fm(g("norm_g").reshape(12, D)),
            fm(g("gla_b_alpha")[0]), fm(g("rg_conv_w")[0]), fm(g("rg_conv_b")[0]), fm(g("rg_b_a")[0]),
            fm(g("rg_b_i")[0]), fm(g("rg_lambda")[0]), mlbg], axis=1)
        assert pk.shape == (128, NPK), pk.shape
        m = dict(shared)
        m["x"] = np.ascontiguousarray(x[b])
        m["ctx"] = np.ascontiguousarray(ctx[b])
        m["pk"] = np.ascontiguousarray(pk.astype(np.float32))
        in_maps.append(m)
    return in_maps


_NC = None


def kernel(**inputs):
    global _NC
    if _NC is None:
        _NC = build()
    in_maps = make_in_maps(inputs)
    ncores = int(os.environ.get("MK_CORES", "8"))
    res = run_bass_kernel_spmd(_NC, in_maps[:ncores], core_ids=list(range(ncores)))
    outs = [np.asarray(r["out"], dtype=np.float32) for r in res.results]
    while len(outs) < 8:
        outs.append(np.zeros_like(outs[0]))
    return np.stack(outs, axis=0)
```
